# Optimizing a Trainium2 kernel written in Bass

```python
import math
import jax, jax.numpy as jnp
from jax import lax
import numpy as np

D_MODEL = 1024
BATCH = 8
SEQ = 4096
DEPTH = 2

ATT_HEADS = 4
ATT_QK_DIM = 64
ATT_V_DIM = 2 * ATT_QK_DIM
ATT_WIDTH = ATT_HEADS * ATT_V_DIM
Q_BLOCK = 128
NEG_LOGIT = -1e30
REL_BUCKETS = 32
REL_MAX_DIST = 128
LRU_WIDTH = 512
LRU_BLOCKS = 8
LRU_BLOCK_W = LRU_WIDTH // LRU_BLOCKS
LRU_CONV = 4
LRU_C = 8.0
SC_WIDTH = 512
SC_CONV = 3
CF_WIDTH = 512
CF_CONV = 31
N_BRANCH = 4
BRANCH_WIDTH = 512
FFN_HIDDEN = -(-8 * D_MODEL // (3 * 256)) * 256
PLE_DIM = 256
EPS = 1e-6

IN_WIDTHS = (ATT_WIDTH, ATT_WIDTH, ATT_WIDTH,
             LRU_WIDTH, LRU_WIDTH,
             SC_WIDTH, SC_WIDTH, SC_WIDTH,
             CF_WIDTH, CF_WIDTH,
             N_BRANCH * D_MODEL)
IN_TOTAL = sum(IN_WIDTHS)

kernel_name = "hybrid_parallel_gated_mixers"


def rmsnorm(x, g):
    xf = x.astype(jnp.float32)
    y = xf * lax.rsqrt(jnp.mean(xf * xf, axis=-1, keepdims=True) + EPS)
    return (y * g.astype(jnp.float32)).astype(x.dtype)


def layernorm(x, g, b):
    xf = x.astype(jnp.float32)
    mu = jnp.mean(xf, axis=-1, keepdims=True)
    var = jnp.mean(jnp.square(xf - mu), axis=-1, keepdims=True)
    y = (xf - mu) * lax.rsqrt(var + EPS)
    return (y * g.astype(jnp.float32) + b.astype(jnp.float32)).astype(x.dtype)


def causal_dwconv(x, w):
    k = w.shape[0]
    return lax.conv_general_dilated(
        x, w[:, None, :].astype(x.dtype), window_strides=(1,), padding=[(k - 1, 0)],
        dimension_numbers=("NWC", "WIO", "NWC"), feature_group_count=x.shape[-1])


def t5_bucket(rel):
    n = jnp.maximum(rel, 0)
    max_exact = REL_BUCKETS // 2
    nf = jnp.maximum(n, 1).astype(jnp.float32)
    large = max_exact + (jnp.log(nf / max_exact) / math.log(REL_MAX_DIST / max_exact)
                         * (REL_BUCKETS - max_exact)).astype(jnp.int32)
    large = jnp.minimum(large, REL_BUCKETS - 1)
    return jnp.where(n < max_exact, n, large)


def diff_attention(q, k, v, rel_bias, lam, sub_g, lam_init):
    b, s, _ = q.shape
    nb = s // Q_BLOCK
    q = q.reshape(b, s, ATT_HEADS, ATT_V_DIM)
    k = k.reshape(b, s, ATT_HEADS, ATT_V_DIM)
    v = v.reshape(b, s, ATT_HEADS, ATT_V_DIM)
    k1, k2 = k[..., :ATT_QK_DIM], k[..., ATT_QK_DIM:]
    scale = ATT_QK_DIM ** -0.5
    qb = q.reshape(b, nb, Q_BLOCK, ATT_HEADS, ATT_V_DIM).transpose(1, 0, 2, 3, 4)
    bias_table = rel_bias.astype(jnp.float32)
    kpos = jnp.arange(s)

    def block(args):
        qblk, bi = args
        qpos = bi * Q_BLOCK + jnp.arange(Q_BLOCK)
        rel = qpos[:, None] - kpos[None, :]
        bias = jnp.transpose(bias_table[t5_bucket(rel)], (2, 0, 1))
        causal = rel >= 0

        def softmax_map(qh, kh):
            sc = jnp.einsum("bqhd,bkhd->bhqk", qh, kh).astype(jnp.float32) * scale + bias
            sc = jnp.where(causal, sc, NEG_LOGIT)
            return jax.nn.softmax(sc, axis=-1)

        probs = softmax_map(qblk[..., :ATT_QK_DIM], k1) - lam * softmax_map(qblk[..., ATT_QK_DIM:], k2)
        return jnp.einsum("bhqk,bkhd->bqhd", probs.astype(v.dtype), v)

    o = lax.map(block, (qb, jnp.arange(nb)))
    o = o.transpose(1, 0, 2, 3, 4).reshape(b, s, ATT_HEADS, ATT_V_DIM)
    o = rmsnorm(o, sub_g) * (1.0 - lam_init)
    return o.reshape(b, s, ATT_WIDTH)


def rglru_branch(xr, gate_in, conv_w, conv_b, wa, ba, wx, bx, lam_param):
    b, s, _ = xr.shape
    xc = causal_dwconv(xr, conv_w) + conv_b.astype(xr.dtype)
    xb = xc.reshape(b, s, LRU_BLOCKS, LRU_BLOCK_W)
    r = jax.nn.sigmoid(jnp.einsum("bshi,hij->bshj", xb, wa) + ba).reshape(b, s, LRU_WIDTH)
    i = jax.nn.sigmoid(jnp.einsum("bshi,hij->bshj", xb, wx) + bx).reshape(b, s, LRU_WIDTH)
    log_a = -LRU_C * r.astype(jnp.float32) * jax.nn.softplus(-lam_param.astype(jnp.float32))
    a = jnp.exp(log_a)
    mult = jnp.sqrt(-jnp.expm1(2.0 * log_a))
    u = mult * (i * xc).astype(jnp.float32)

    def combine(left, right):
        a1, b1 = left
        a2, b2 = right
        return a1 * a2, a2 * b1 + b2

    _, h = lax.associative_scan(combine, (a, u), axis=1)
    return h.astype(xr.dtype) * jax.nn.gelu(gate_in)


def short_conv_branch(bg, cg, xs, conv_w):
    return bg * causal_dwconv(cg * xs, conv_w)


def conformer_branch(va, vg, conv_w, conv_b, ln_g, ln_b):
    u = va * jax.nn.sigmoid(vg)
    u = causal_dwconv(u, conv_w) + conv_b.astype(u.dtype)
    return jax.nn.silu(layernorm(u, ln_g, ln_b))


def setup_inputs(seed: int = 0) -> dict:
    key = jax.random.key(seed)
    ks = iter(jax.random.split(key, 40))
    f32 = jnp.float32

    def nrm(shape, scale):
        return jax.random.normal(next(ks), shape, f32) * scale

    def gain(shape):
        return 1.0 + nrm(shape, 0.02)

    u = jax.random.uniform(next(ks), (DEPTH, LRU_WIDTH), f32, 0.9, 0.999)
    a0 = u ** (1.0 / LRU_C)
    lru_lambda = jnp.log(a0) - jnp.log1p(-a0)
    return {
        "x": nrm((BATCH, SEQ, D_MODEL), 1.0),
        "p": nrm((DEPTH, BATCH, SEQ, PLE_DIM), 1.0),
        "rel_bias": nrm((REL_BUCKETS, ATT_HEADS), 0.5),
        "g_pre_mix": gain((DEPTH, D_MODEL)),
        "w_in": nrm((DEPTH, D_MODEL, IN_TOTAL), D_MODEL ** -0.5),
        "att_lambda": nrm((DEPTH, 4, ATT_QK_DIM), 0.1),
        "att_subnorm_g": gain((DEPTH, ATT_V_DIM)),
        "lru_conv_w": nrm((DEPTH, LRU_CONV, LRU_WIDTH), LRU_CONV ** -0.5),
        "lru_conv_b": nrm((DEPTH, LRU_WIDTH), 0.02),
        "lru_wa": nrm((DEPTH, LRU_BLOCKS, LRU_BLOCK_W, LRU_BLOCK_W), LRU_BLOCK_W ** -0.5),
        "lru_ba": nrm((DEPTH, LRU_BLOCKS, LRU_BLOCK_W), 0.02),
        "lru_wx": nrm((DEPTH, LRU_BLOCKS, LRU_BLOCK_W, LRU_BLOCK_W), LRU_BLOCK_W ** -0.5),
        "lru_bx": nrm((DEPTH, LRU_BLOCKS, LRU_BLOCK_W), 0.02),
        "lru_lambda": lru_lambda,
        "sc_conv_w": nrm((DEPTH, SC_CONV, SC_WIDTH), SC_CONV ** -0.5),
        "cf_conv_w": nrm((DEPTH, CF_CONV, CF_WIDTH), CF_CONV ** -0.5),
        "cf_conv_b": nrm((DEPTH, CF_WIDTH), 0.02),
        "cf_ln_g": gain((DEPTH, CF_WIDTH)),
        "cf_ln_b": nrm((DEPTH, CF_WIDTH), 0.02),
        "gate_b": nrm((DEPTH, N_BRANCH, D_MODEL), 0.02),
        "w_branch": nrm((DEPTH, N_BRANCH, BRANCH_WIDTH, D_MODEL), BRANCH_WIDTH ** -0.5),
        "w_o": nrm((DEPTH, D_MODEL, D_MODEL), D_MODEL ** -0.5),
        "g_post_mix": gain((DEPTH, D_MODEL)),
        "g_pre_ffn": gain((DEPTH, D_MODEL)),
        "w_ffn_in": nrm((DEPTH, D_MODEL, 2 * FFN_HIDDEN), D_MODEL ** -0.5),
        "w_ffn_out": nrm((DEPTH, FFN_HIDDEN, D_MODEL), FFN_HIDDEN ** -0.5),
        "g_post_ffn": gain((DEPTH, D_MODEL)),
        "w_ple_in": nrm((DEPTH, PLE_DIM, D_MODEL), PLE_DIM ** -0.5),
        "g_ple": gain((DEPTH, D_MODEL)),
        "w_ple_gate": nrm((DEPTH, D_MODEL, D_MODEL), D_MODEL ** -0.5),
    }


def reference(x, p, rel_bias, g_pre_mix, w_in, att_lambda, att_subnorm_g, lru_conv_w, lru_conv_b,
              lru_wa, lru_ba, lru_wx, lru_bx, lru_lambda, sc_conv_w, cf_conv_w, cf_conv_b, cf_ln_g,
              cf_ln_b, gate_b, w_branch, w_o, g_post_mix, g_pre_ffn, w_ffn_in, w_ffn_out, g_post_ffn,
              w_ple_in, g_ple, w_ple_gate):
    b, s, d = x.shape
    split_points = []
    acc = 0
    for wdt in IN_WIDTHS[:-1]:
        acc += wdt
        split_points.append(acc)

    for l in range(DEPTH):
        h = rmsnorm(x, g_pre_mix[l])
        z = h @ w_in[l]
        (q, k, v, lru_x, lru_g, sc_b, sc_c, sc_x, cf_a, cf_g, gate_logits) = jnp.split(z, split_points, axis=-1)

        lam_init = 0.8 - 0.6 * math.exp(-0.3 * l)
        lv = att_lambda[l].astype(jnp.float32)
        lam = jnp.exp(jnp.sum(lv[0] * lv[1])) - jnp.exp(jnp.sum(lv[2] * lv[3])) + lam_init

        y_att = diff_attention(q, k, v, rel_bias, lam, att_subnorm_g[l], lam_init)
        y_lru = rglru_branch(lru_x, lru_g, lru_conv_w[l], lru_conv_b[l], lru_wa[l], lru_ba[l],
                             lru_wx[l], lru_bx[l], lru_lambda[l])
        y_sc = short_conv_branch(sc_b, sc_c, sc_x, sc_conv_w[l])
        y_cf = conformer_branch(cf_a, cf_g, cf_conv_w[l], cf_conv_b[l], cf_ln_g[l], cf_ln_b[l])

        gates = jax.nn.sigmoid(gate_logits.reshape(b, s, N_BRANCH, d) + gate_b[l])
        merged = gates[:, :, 0] * (y_att @ w_branch[l, 0])
        merged = merged + gates[:, :, 1] * (y_lru @ w_branch[l, 1])
        merged = merged + gates[:, :, 2] * (y_sc @ w_branch[l, 2])
        merged = merged + gates[:, :, 3] * (y_cf @ w_branch[l, 3])
        x = x + rmsnorm(merged @ w_o[l], g_post_mix[l])

        h2 = rmsnorm(x, g_pre_ffn[l])
        gt, up = jnp.split(h2 @ w_ffn_in[l], 2, axis=-1)
        x = x + rmsnorm((jax.nn.silu(gt) * up) @ w_ffn_out[l], g_post_ffn[l])

        e = p[l] @ w_ple_in[l]
        ge = jax.nn.sigmoid(rmsnorm(x, g_ple[l]) @ w_ple_gate[l])
        x = x + ge * e
    return x
```

```python
import math
import os
from contextlib import ExitStack

import numpy as np

import concourse.bass as bass
import concourse.mybir as mybir
from concourse.bass_utils import run_bass_kernel_spmd

F32 = mybir.dt.float32
BF16 = mybir.dt.bfloat16
AF = mybir.ActivationFunctionType
ALU = mybir.AluOpType
AX = mybir.AxisListType

S = 4096
D = 1024
TT = 512
NT = S // TT
NCH = D // 128
IN_TOTAL = 9216
FFH = 2816
NJ = FFH // 128
EPS = 1e-6
NCORES = 8
FUSED = True
DEBUG = bool(int(os.environ.get("MK_DEBUG", "0")))

ENGINES = ("sync", "act", "pool", "dve", "pe")
SEM_ROT = int(os.environ.get("MK_SEMROT", 30000))


class Buf:
    __slots__ = ("name", "lastw", "reads", "dsem", "dcnt")

    def __init__(self, name):
        self.name = name
        self.lastw = {}
        self.reads = {}
        self.dsem = None
        self.dcnt = 0


class Prog:
    def __init__(self, nc, stack, strict=True):
        self.nc = nc
        self.stack = stack
        self.ops = {e: [] for e in ENGINES}
        self.esem = {}
        self.ecnt = {}
        self.seen = {e: {} for e in ENGINES}
        self.strict = strict
        self.nsem = 0
        self.n_wait = 0
        self.n_op = 0
        self.n_mm = 0
        for e in ENGINES:
            self._new_esem(e)

    def _alloc_sem(self, name):
        self.nsem += 1
        return self.stack.enter_context(self.nc.semaphore(f"{name}_{self.nsem}"))

    def _new_esem(self, e):
        self.esem[e] = self._alloc_sem("e" + e)
        self.ecnt[e] = 0

    def _deps(self, eng, reads, writes):
        need = {}
        for b in reads:
            for s, v in b.lastw.items():
                if need.get(s, 0) < v:
                    need[s] = v
        for b in writes:
            for s, v in b.lastw.items():
                if need.get(s, 0) < v:
                    need[s] = v
            for s, v in b.reads.items():
                if need.get(s, 0) < v:
                    need[s] = v
        out = []
        seen = self.seen[eng]
        own = self.esem[eng]
        for s, v in need.items():
            if s is own and (eng == "pe" or not self.strict):
                continue
            if seen.get(s, 0) >= v:
                continue
            seen[s] = v
            out.append((s, v))
        return out

    def _emit_waits(self, eng, deps):
        for s, v in deps:
            self.n_wait += 1
            self.ops[eng].append(lambda e, s=s, v=v: e.wait_ge(s, v))

    def op(self, eng, fn, reads=(), writes=()):
        deps = self._deps(eng, reads, writes)
        self._emit_waits(eng, deps)
        if self.ecnt[eng] >= SEM_ROT:
            self._new_esem(eng)
        sem = self.esem[eng]
        self.ecnt[eng] += 1
        val = self.ecnt[eng]
        self.n_op += 1
        self.ops[eng].append(lambda e, fn=fn, sem=sem: fn(e).then_inc(sem, 1))
        for b in writes:
            b.lastw = {sem: val}
            b.reads = {}
        for b in reads:
            if b not in writes:
                b.reads[sem] = val

    def dma(self, eng, pairs, reads=(), writes=()):
        deps = self._deps(eng, reads, writes)
        self._emit_waits(eng, deps)
        owner = writes[0]
        if owner.dsem is None or owner.dcnt + 16 * len(pairs) > SEM_ROT:
            owner.dsem = self._alloc_sem("d")
            owner.dcnt = 0
        sem = owner.dsem
        for (o, i) in pairs:
            owner.dcnt += 16
            self.n_op += 1
            self.ops[eng].append(lambda e, o=o, i=i, sem=sem: e.dma_start(out=o, in_=i).then_inc(sem, 16))
        val = owner.dcnt
        for b in writes:
            b.lastw = {sem: val}
            b.reads = {}
        for b in reads:
            if b not in writes:
                b.reads[sem] = val

    def wait_all(self, eng, bufs):
        self._emit_waits(eng, self._deps(eng, bufs, ()))

    def run(self):
        ops = self.ops
        with self.nc.Block() as block:
            @block.sync
            def _(e):
                for f in ops["sync"]:
                    f(e)

            @block.scalar
            def _(e):
                for f in ops["act"]:
                    f(e)

            @block.gpsimd
            def _(e):
                for f in ops["pool"]:
                    f(e)

            @block.vector
            def _(e):
                for f in ops["dve"]:
                    f(e)

            @block.tensor
            def _(e):
                for f in ops["pe"]:
                    f(e)


class Unit:
    __slots__ = ("t", "B")

    def __init__(self, t, B):
        self.t = t
        self.B = B


class UnitPool:
    def __init__(self, nc, st, name, n, shape, dtype, psum=False):
        self.free = []
        self.name = name
        for i in range(n):
            mk = nc.psum_tensor if psum else nc.sbuf_tensor
            t = st.enter_context(mk(f"{name}{i}", shape, dtype))
            self.free.append(Unit(t, Buf(f"{name}{i}")))
        self.n = n
        self.minfree = n

    def alloc(self):
        assert self.free, f"pool {self.name} exhausted"
        u = self.free.pop(0)
        self.minfree = min(self.minfree, len(self.free))
        return u

    def release(self, u):
        self.free.append(u)


VEC_LAYOUT = [("g_pre_mix", 8), ("g_post_mix", 8), ("g_pre_ffn", 8), ("g_post_ffn", 8), ("g_ple", 8),
              ("gate_b", 32), ("lru_cw", 16), ("lru_cb", 4), ("lru_ba", 4), ("lru_bx", 4), ("lru_lam", 4),
              ("sc_cw", 12), ("cf_cw", 124), ("cf_cb", 4), ("cf_g", 4), ("cf_b", 4), ("sub_g", 1),
              ("att_lam", 256)]
VOFF = {}
_c = 0
for _n, _w in VEC_LAYOUT:
    VOFF[_n] = _c
    _c += _w
NV = _c


def _fm(v):
    v = np.asarray(v, np.float32)
    return np.ascontiguousarray(v.reshape(-1, 128).T)


def _t5_bucket_np(n):
    n = np.maximum(n, 0)
    nf = np.maximum(n, 1).astype(np.float32)
    large = 16 + (np.log(nf / np.float32(16)) / np.float32(math.log(128 / 16)) * np.float32(16)).astype(np.int32)
    large = np.minimum(large, 31)
    return np.where(n < 16, n, large)


def _consts():
    d = np.arange(384) - 127
    bk = _t5_bucket_np(d)
    onehot = np.zeros((32, 384), np.float32)
    for i in range(384):
        if d[i] >= 0:
            onehot[bk[i], i] = 1.0
    mask = np.where(d < 0, -30000.0, 0.0).astype(np.float32)[None, :]
    J = np.zeros((128, 128), np.float32)
    J[np.arange(128), 127 - np.arange(128)] = 1.0
    return onehot, mask, J


def _pack_vecs(inp, l):
    V = np.zeros((128, NV), np.float32)

    def put(name, arr):
        arr = np.asarray(arr, np.float32)
        V[:, VOFF[name]:VOFF[name] + arr.shape[1]] = arr

    put("g_pre_mix", _fm(inp["g_pre_mix"][l]))
    put("g_post_mix", _fm(inp["g_post_mix"][l]))
    put("g_pre_ffn", _fm(inp["g_pre_ffn"][l]))
    put("g_post_ffn", _fm(inp["g_post_ffn"][l]))
    put("g_ple", _fm(inp["g_ple"][l]))
    put("gate_b", np.concatenate([_fm(inp["gate_b"][l, b]) for b in range(4)], axis=1))
    def convw(w):
        K = w.shape[0]
        w = np.asarray(w, np.float32).reshape(K, -1, 128)
        return np.ascontiguousarray(w.transpose(2, 1, 0).reshape(128, -1))
    put("lru_cw", convw(inp["lru_conv_w"][l]))
    put("lru_cb", _fm(inp["lru_conv_b"][l]))
    put("lru_ba", _fm(np.asarray(inp["lru_ba"][l]).reshape(-1)))
    put("lru_bx", _fm(np.asarray(inp["lru_bx"][l]).reshape(-1)))
    put("lru_lam", _fm(inp["lru_lambda"][l]))
    put("sc_cw", convw(inp["sc_conv_w"][l]))
    put("cf_cw", convw(inp["cf_conv_w"][l]))
    put("cf_cb", _fm(inp["cf_conv_b"][l]))
    put("cf_g", _fm(inp["cf_ln_g"][l]))
    put("cf_b", _fm(inp["cf_ln_b"][l]))
    put("sub_g", np.asarray(inp["att_subnorm_g"][l], np.float32).reshape(128, 1))
    put("att_lam", np.broadcast_to(np.asarray(inp["att_lambda"][l], np.float32).reshape(1, 256), (128, 256)))
    return V


def _pack_lru_bd(inp, l):
    out = np.zeros((2, 4, 128, 128), np.float32)
    for g, name in enumerate(("lru_wa", "lru_wx")):
        w = np.asarray(inp[name][l], np.float32)
        for c in range(4):
            out[g, c, 0:64, 0:64] = w[2 * c]
            out[g, c, 64:128, 64:128] = w[2 * c + 1]
    return out


def build_program(NL, lam_inits, debug=False):
    nc = bass.Bass("TRN2", target_bir_lowering=False)

    def din(name, shape, dt=F32):
        return nc.dram_tensor(name, list(shape), dt, kind="ExternalInput")

    xT_h = din("xT", [D, S])
    pT_h = din("pT", [NL, 256, S])
    relb_h = din("rel_bias", [32, 4])
    onehot_h = din("c_onehot", [32, 384])
    mask_h = din("c_mask", [1, 384])
    J_h = din("c_J", [128, 128])
    vecs_h = din("vecs", [NL, 128, NV])
    lrubd_h = din("lru_bd", [NL, 2, 4, 128, 128])
    w_in_h = din("w_in", [NL, D, IN_TOTAL])
    w_br_h = din("w_branch", [NL, 2048, D])
    w_o_h = din("w_o", [NL, D, D])
    w_fi_h = din("w_ffn_in", [NL, D, 2 * FFH])
    w_fo_h = din("w_ffn_out", [NL, FFH, D])
    w_pi_h = din("w_ple_in", [NL, 256, D])
    w_pg_h = din("w_ple_gate", [NL, D, D])
    out_h = nc.dram_tensor("outT", [D, S], F32, kind="ExternalOutput")

    def dint(name, shape, dt):
        return nc.dram_tensor(name, list(shape), dt, kind="Internal")

    xmid_h = dint("xmid", [D, S], F32) if NL > 1 else None
    vecD_h = dint("vecD", [4, 384], F32)
    NG = 45
    wsc = {l: dint(f"wsc{l}", [NG, 128, 4096], BF16) for l in range(NL)}
    wsrc = {l: dict(w_in=w_in_h.ap()[l], w_br=w_br_h.ap()[l], w_o=w_o_h.ap()[l], w_fi=w_fi_h.ap()[l],
                    w_fo=w_fo_h.ap()[l], w_pi=w_pi_h.ap()[l], w_pg=w_pg_h.ap()[l]) for l in range(NL)}
    dbg_h = None
    if debug:
        dbg_h = nc.dram_tensor("dbg", [NL, 4, 128, 4, S], BF16, kind="ExternalOutput")

    with ExitStack() as st:
        P = Prog(nc, st, strict=True)

        def sb(name, shape, dt):
            return st.enter_context(nc.sbuf_tensor(name, list(shape), dt))

        KT = sb("KT", [128, 4, S], BF16)
        VC = sb("VC", [128, S // 128, 512], BF16)
        BKT = [Buf(f"KT{t}") for t in range(NT)]
        BV = [Buf(f"V{b}") for b in range(S // 128)]
        CI = sb("CI", [128, 4, 30 + TT], F32)
        BCI = [Buf(f"CI{c}") for c in range(4)]
        HL = sb("HL", [128, 4, 3], F32)
        HS = sb("HS", [128, 4, 2], F32)
        HC = sb("HC", [128, 4, 30], F32)
        BHL, BHS, BHC = Buf("HL"), Buf("HS"), Buf("HC")
        HST = sb("HST", [128, 4], F32)
        BHST = [Buf(f"HST{c}") for c in range(4)]
        VEC = sb("VEC", [128, NL, NV], F32)
        BVEC = Buf("VEC")
        DER = sb("DER", [128, NL, 16], F32)
        BDER = Buf("DER")
        CST = sb("CST", [128, 4], F32)
        BCST = Buf("CST")
        ONES = sb("ONES", [128, 128], BF16)
        BONES = Buf("ONES")
        BT = sb("BT", [128, 4, 256], F32)
        BBT = Buf("BT")
        BFAR = sb("BFAR", [128, 4], F32)
        BBFAR = Buf("BFAR")
        LW = sb("LW", [128, NL, 8, 128], BF16)
        BLW = Buf("LW")
        NSLOT = 5
        SLOTS = [Unit(sb(f"slot{i}", [128, 4096], BF16), Buf(f"slot{i}")) for i in range(NSLOT)]

        FP = UnitPool(nc, st, "F", 22, [128, TT], F32)
        BP = UnitPool(nc, st, "B", 34, [128, TT], BF16)
        PSP = UnitPool(nc, st, "PS", 8, [128, TT], F32, psum=True)

        def MM(ps_ap, lhsT, rhs, start, stop, reads, psB):
            P.n_mm += 1
            P.op("pe", lambda e: e.matmul(ps_ap, lhsT=lhsT, rhs=rhs, start=start, stop=stop), reads=reads, writes=[psB])

        def ACT(out, in_, func, reads, writes, bias=None, scale=None):
            kw = {}
            if bias is not None:
                kw["bias"] = bias
            if scale is not None:
                kw["scale"] = scale
            P.op("act", lambda e: e.activation(out=out, in_=in_, func=func, **kw), reads=reads, writes=writes)

        def TTo(eng, out, in0, in1, op, reads, writes):
            P.op(eng, lambda e: e.tensor_tensor(out=out, in0=in0, in1=in1, op=op), reads=reads, writes=writes)

        def TS(eng, out, in0, s1, s2, op0, op1, reads, writes):
            if op1 is None:
                P.op(eng, lambda e: e.tensor_scalar(out=out, in0=in0, scalar1=s1, scalar2=None, op0=op0),
                     reads=reads, writes=writes)
            else:
                P.op(eng, lambda e: e.tensor_scalar(out=out, in0=in0, scalar1=s1, scalar2=s2, op0=op0, op1=op1),
                     reads=reads, writes=writes)

        def STT(out, in0, scalar, in1, op0, op1, reads, writes):
            P.op("dve", lambda e: e.scalar_tensor_tensor(out=out, in0=in0, scalar=scalar, in1=in1, op0=op0, op1=op1),
                 reads=reads, writes=writes)

        def CP(eng, out, in_, reads, writes):
            if eng == "act":
                P.op("act", lambda e: e.activation(out=out, in_=in_, func=AF.Copy), reads=reads, writes=writes)
            else:
                P.op(eng, lambda e: e.tensor_copy(out=out, in_=in_), reads=reads, writes=writes)

        def SCAN(out, d0, d1, init, reads, writes):
            P.op("dve", lambda e: e.tensor_tensor_scan(out=out, data0=d0, data1=d1, initial=init,
                                                       op0=ALU.mult, op1=ALU.add), reads=reads, writes=writes)

        def RECIP(out, in_, reads, writes):
            P.op("dve", lambda e: e.reciprocal(out=out, in_=in_), reads=reads, writes=writes)

        GSETS = [3, 2, 2, 3, 6, 6, 2, 12, 6, 3]
        assert sum(GSETS) == NG
        BWG = {}
        for l in range(NL):
            BWG[l] = []
            for si, n in enumerate(GSETS):
                b = Buf(f"wg{l}_{si}")
                BWG[l] += [b] * n

        def group_list(l):
            W = wsrc[l]
            sc = []
            win = lambda tag, chunk0: (tag, W["w_in"], 0, 8, chunk0 * 128, 512)
            sc += [win("q", 0), win("k", 4), win("v", 8), win("cg", 36), win("ca", 32)]
            sc += [win("lx", 12), win("lg", 16), win("sx", 28), win("sc", 24), win("sb", 20)]
            for og in range(2):
                for b in range(4):
                    if b % 2 == 0:
                        sc.append(("br", W["w_br"], (b // 2) * 1024, 8, og * 512, 512))
                    sc.append(win("gate", 40 + b * 8 + og * 4))
            for og in range(2):
                sc.append(("wo", W["w_o"], 0, 8, og * 512, 512))
            for jg in range(6):
                ncols = 512 if jg < 5 else 256
                sc.append(("fg", W["w_fi"], 0, 8, jg * 512, ncols))
                sc.append(("fu", W["w_fi"], 0, 8, FFH + jg * 512, ncols))
            for og in range(2):
                for kg in range(3):
                    nk = 8 if kg < 2 else 6
                    sc.append(("fo", W["w_fo"], kg * 1024, nk, og * 512, 512))
            sc.append(("pi", W["w_pi"], 0, 2, 0, 1024))
            for og in range(2):
                sc.append(("pg", W["w_pg"], 0, 8, og * 512, 512))
            assert len(sc) == NG
            return sc

        GL = {l: group_list(l) for l in range(NL)}

        def convert_layer(l):
            g = 0
            for n in GSETS:
                pairs = []
                for _ in range(n):
                    tag, src, r0, nk, c0, ncols = GL[l][g]
                    s_ap = src[r0:r0 + nk * 128, c0:c0 + ncols].rearrange("(k p) n -> p k n", p=128)
                    d_ap = wsc[l].ap()[g, :, 0:nk * ncols].rearrange("p (k n) -> p k n", k=nk)
                    pairs.append((d_ap, s_ap))
                    g += 1
                P.dma("pool", pairs, writes=[BWG[l][g - 1]])

        P.dma("sync", [(VEC[:, l, :], vecs_h.ap()[l]) for l in range(NL)], writes=[BVEC])
        P.op("dve", lambda e: e.memset(CST[:, 0:1], EPS), writes=[BCST])
        P.op("dve", lambda e: e.memset(ONES[:, :], 1.0), writes=[BONES])
        P.dma("pool", [(LW[:, l, :, :], lrubd_h.ap()[l].rearrange("g c i j -> i (g c) j")) for l in range(NL)],
              writes=[BLW])
        convert_layer(0)
        if NL > 1:
            convert_layer(1)

        u1, u2, u3 = FP.alloc(), FP.alloc(), FP.alloc()
        P.dma("sync", [(u1.t[0:32, 0:4], relb_h.ap()), (u1.t[0:32, 8:392], onehot_h.ap())], writes=[u1.B])
        P.dma("sync", [(u2.t[0:1, 8:392], mask_h.ap()), (u3.t[:, 0:128], J_h.ap())], writes=[u2.B, u3.B])
        P.op("dve", lambda e: e.memset(u2.t[0:1, 0:4], 1.0), reads=[], writes=[u2.B])
        psb = PSP.alloc()
        MM(psb.t[0:4, 0:384], u1.t[0:32, 0:4], u1.t[0:32, 8:392], True, False, [u1.B], psb.B)
        MM(psb.t[0:4, 0:384], u2.t[0:1, 0:4], u2.t[0:1, 8:392], False, True, [u2.B], psb.B)
        u4 = FP.alloc()
        CP("dve", u4.t[0:4, 0:384], psb.t[0:4, 0:384], [psb.B], [u4.B])
        PSP.release(psb)
        BvecD = Buf("vecD")
        P.dma("sync", [(vecD_h.ap(), u4.t[0:4, 0:384])], reads=[u4.B], writes=[BvecD])
        for h in range(4):
            hk = FP.alloc()
            P.dma("sync", [(hk.t[:, 0:256], bass.AP(vecD_h, h * 384, [[1, 128], [1, 256]]))], reads=[BvecD],
                  writes=[hk.B])
            psb = PSP.alloc()
            MM(psb.t[:, 0:256], u3.t[:, 0:128], hk.t[:, 0:256], True, True, [u3.B, hk.B], psb.B)
            CP("dve", BT[:, h, :], psb.t[:, 0:256], [psb.B], [BBT])
            PSP.release(psb)
            FP.release(hk)
            CP("dve", BFAR[:, h:h + 1], BT[:, h, 255:256], [BBT], [BBFAR])
            TS("dve", BT[:, h, :], BT[:, h, :], BFAR[:, h:h + 1], None, ALU.subtract, None, [BBT, BBFAR], [BBT])
        for u in (u1, u2, u3, u4):
            FP.release(u)

        for l in range(NL):
            vo = lambda name, i=0, l=l: VEC[:, l, VOFF[name] + i:VOFF[name] + i + 1]
            tmp = FP.alloc()
            la = VOFF["lru_lam"]
            ACT(tmp.t[:, 0:4], VEC[:, l, la:la + 4], AF.Exp, [BVEC], [tmp.B], scale=-1.0)
            TS("dve", tmp.t[:, 0:4], tmp.t[:, 0:4], 1.0, None, ALU.add, None, [tmp.B], [tmp.B])
            ACT(tmp.t[:, 4:8], tmp.t[:, 0:4], AF.Ln, [tmp.B], [tmp.B])
            TS("dve", DER[:, l, 0:4], tmp.t[:, 4:8], -8.0, None, ALU.mult, None, [tmp.B], [BDER])
            TS("dve", DER[:, l, 4:8], tmp.t[:, 4:8], -16.0, None, ALU.mult, None, [tmp.B], [BDER])
            al = VOFF["att_lam"]
            TTo("dve", tmp.t[:, 16:80], VEC[:, l, al:al + 64], VEC[:, l, al + 64:al + 128], ALU.mult, [BVEC], [tmp.B])
            TTo("dve", tmp.t[:, 80:144], VEC[:, l, al + 128:al + 192], VEC[:, l, al + 192:al + 256], ALU.mult,
                [BVEC], [tmp.B])
            P.op("dve", lambda e, tmp=tmp: e.reduce_sum(out=tmp.t[:, 8:9], in_=tmp.t[:, 16:80], axis=AX.X),
                 reads=[tmp.B], writes=[tmp.B])
            P.op("dve", lambda e, tmp=tmp: e.reduce_sum(out=tmp.t[:, 9:10], in_=tmp.t[:, 80:144], axis=AX.X),
                 reads=[tmp.B], writes=[tmp.B])
            ACT(tmp.t[:, 10:12], tmp.t[:, 8:10], AF.Exp, [tmp.B], [tmp.B])
            TTo("dve", tmp.t[:, 12:13], tmp.t[:, 11:12], tmp.t[:, 10:11], ALU.subtract, [tmp.B], [tmp.B])
            TS("dve", DER[:, l, 8:9], tmp.t[:, 12:13], -float(lam_inits[l]), None, ALU.add, None, [tmp.B], [BDER])
            TS("dve", DER[:, l, 9:10], vo("sub_g"), 1.0 - float(lam_inits[l]), None, ALU.mult, None, [BVEC], [BDER])
            FP.release(tmp)

        class WStream:
            def __init__(self):
                self.sched = []
                self.pos = 0
                self.issued = 0
                self.busy = [False] * NSLOT

            def _issue(self, i):
                tag, l, g, n = self.sched[i]
                slot = SLOTS[i % NSLOT]
                P.dma("sync", [(slot.t[:, 0:n], wsc[l].ap()[g, :, 0:n])], reads=[BWG[l][g]], writes=[slot.B])

            def pump(self):
                while self.issued < len(self.sched) and not self.busy[self.issued % NSLOT]:
                    self._issue(self.issued)
                    self.busy[self.issued % NSLOT] = True
                    self.issued += 1

            def next(self, tag):
                self.pump()
                assert self.issued > self.pos, "weight slots deadlock"
                ent = self.sched[self.pos]
                assert ent[0] == tag, (self.pos, ent[0], tag)
                slot = SLOTS[self.pos % NSLOT]
                self.pos += 1
                return slot

            def done(self, slot):
                i = SLOTS.index(slot)
                assert self.busy[i]
                self.busy[i] = False
                self.pump()

        WS = WStream()

        def sched_tile(l):
            return [(tag, l, g, nk * ncols) for g, (tag, src, r0, nk, c0, ncols) in enumerate(GL[l])]

        NTE = int(os.environ.get("MK_NTILES", NT))
        for l in range(NL):
            for t in range(NTE):
                WS.sched.extend(sched_tile(l))
        def WNEXT(tag):
            return WS.next(tag)

        WDONE = WS.done

        xs = [FP.alloc() for _ in range(NCH)]
        BXMID = [Buf(f"xmid{t}") for t in range(NT)]
        BOUT = [Buf(f"out{t}") for t in range(NT)]

        def rms_rstd(srcs, inv_n):
            psu = PSP.alloc()
            n = len(srcs)
            for c, (ap, b) in enumerate(srcs):
                sq = BP.alloc()
                ACT(sq.t[:, :], ap, AF.Square, [b], [sq.B])
                MM(psu.t[:, :], ONES[:, :], sq.t[:, :], c == 0, c == n - 1, [sq.B, BONES], psu.B)
                BP.release(sq)
            r = FP.alloc()
            ACT(r.t[:, :], psu.t[:, :], AF.Sqrt, [psu.B, BCST], [r.B], bias=CST[:, 0:1], scale=inv_n)
            PSP.release(psu)
            RECIP(r.t[:, :], r.t[:, :], [r.B], [r.B])
            return r

        def norm_to_bf16(l, gname):
            r = rms_rstd([(xs[c].t[:, :], xs[c].B) for c in range(NCH)], 1.0 / D)
            hb = []
            for c in range(NCH):
                u = BP.alloc()
                STT(u.t[:, :], xs[c].t[:, :], VEC[:, l, VOFF[gname] + c:VOFF[gname] + c + 1], r.t[:, :],
                    ALU.mult, ALU.mult, [xs[c].B, r.B, BVEC], [u.B])
                hb.append(u)
            FP.release(r)
            return hb

        def dense(psu, slot, nk, ncols, j, rhs, k0=0, start=True, stop=True):
            for kc in range(nk):
                MM(psu.t[:, :], slot.t[:, kc * ncols + j * 128:kc * ncols + (j + 1) * 128], rhs[k0 + kc].t[:, :],
                   start and kc == 0, stop and kc == nk - 1, [slot.B, rhs[k0 + kc].B], psu.B)

        def residual_add(l, srcs, gname):
            r = rms_rstd([(u.t[:, :], u.B) for u in srcs], 1.0 / D)
            for c in range(NCH):
                u = srcs[c]
                STT(u.t[:, :], u.t[:, :], VEC[:, l, VOFF[gname] + c:VOFF[gname] + c + 1], r.t[:, :],
                    ALU.mult, ALU.mult, [u.B, r.B, BVEC], [u.B])
                TTo("pool" if c in (1, 4, 7) else "dve", xs[c].t[:, :], xs[c].t[:, :], u.t[:, :], ALU.add,
                    [xs[c].B, u.B], [xs[c].B])
                FP.release(u)
            FP.release(r)

        STOP = int(os.environ.get("MK_STOP", 99))
        MARKS = []
        pe_total = lambda: sum(1 for f in P.ops["pe"])

        def emit_tile(l, t, xin_ap, Bxin, xout_ap, Bxout):
            tsl = slice(t * TT, (t + 1) * TT)
            vcol = lambda name, i: VEC[:, l, VOFF[name] + i:VOFF[name] + i + 1]
            MARKS.append((l, t, "start", P.n_mm))
            P.dma("sync", [(xs[c].t[:, :], xin_ap[c * 128:(c + 1) * 128, tsl]) for c in range(NCH)],
                  reads=[Bxin], writes=[u.B for u in xs])
            hb = norm_to_bf16(l, "g_pre_mix")
            slot = WNEXT("q")
            QT = []
            for hd in range(4):
                psu = PSP.alloc()
                dense(psu, slot, 8, 512, hd, hb)
                qa, qb = BP.alloc(), BP.alloc()
                P.op("pool", lambda e, qa=qa: e.memset(qa.t[64:128, :], 0.0), writes=[qa.B])
                P.op("pool", lambda e, qb=qb: e.memset(qb.t[0:64, :], 0.0), writes=[qb.B])
                CP("act", qa.t[0:64, :], psu.t[0:64, :], [psu.B], [qa.B])
                CP("act", qb.t[64:128, :], psu.t[64:128, :], [psu.B], [qb.B])
                PSP.release(psu)
                QT.append((qa, qb))
            WDONE(slot)
            slot = WNEXT("k")
            for hd in range(4):
                psu = PSP.alloc()
                dense(psu, slot, 8, 512, hd, hb)
                CP("act" if hd % 2 == 1 else "dve", KT[:, hd, tsl], psu.t[:, :], [psu.B], [BKT[t]])
                PSP.release(psu)
            WDONE(slot)
            slot = WNEXT("v")
            for j in range(4):
                psu = PSP.alloc()
                for kc in range(8):
                    MM(psu.t[:, :], hb[kc].t[:, j * 128:(j + 1) * 128], slot.t[:, kc * 512:(kc + 1) * 512],
                       kc == 0, kc == 7, [slot.B, hb[kc].B], psu.B)
                CP("act" if j % 2 == 0 else "dve", VC[:, t * 4 + j, :], psu.t[:, :], [psu.B], [BV[t * 4 + j]])
                PSP.release(psu)
            WDONE(slot)

            if t >= 1 and STOP <= 0:
                return
            MARKS.append((l, t, "attn", P.n_mm))
            if t == 0:
                P.op("pool", lambda e: e.memset(CI[:, :, 0:30], 0.0), writes=BCI)
            else:
                CP("pool", CI[:, :, 0:30], HC[:, :, :], [BHC], BCI)
            slot_g = WNEXT("cg")
            for c in range(4):
                psu = PSP.alloc()
                dense(psu, slot_g, 8, 512, c, hb)
                ACT(CI[:, c, 30:30 + TT], psu.t[:, :], AF.Sigmoid, [psu.B], [BCI[c]])
                PSP.release(psu)
            WDONE(slot_g)
            slot_a = WNEXT("ca")
            for c in range(4):
                psu = PSP.alloc()
                dense(psu, slot_a, 8, 512, c, hb)
                TTo("dve", CI[:, c, 30:30 + TT], psu.t[:, :], CI[:, c, 30:30 + TT], ALU.mult, [psu.B, BCI[c]], [BCI[c]])
                PSP.release(psu)
            WDONE(slot_a)
            CV = []
            pm = PSP.alloc()

            def cf_conv_chunk(c):
                cv = FP.alloc()
                TS("dve", cv.t[:, :], CI[:, c, 0:TT], vcol("cf_cw", c * 31 + 0), vcol("cf_cb", c), ALU.mult, ALU.add,
                   [BCI[c], BVEC], [cv.B])
                for k in range(1, 31):
                    STT(cv.t[:, :], CI[:, c, k:k + TT], vcol("cf_cw", c * 31 + k), cv.t[:, :], ALU.mult, ALU.add,
                        [BCI[c], BVEC, cv.B], [cv.B])
                cvb = BP.alloc()
                CP("pool", cvb.t[:, :], cv.t[:, :], [cv.B], [cvb.B])
                CV.append(cv)
                return cvb

            def cf_mean_mm(c, cvb):
                MM(pm.t[:, :], ONES[:, :], cvb.t[:, :], c == 0, c == 3, [BONES, cvb.B], pm.B)
                BP.release(cvb)

            YA = []
            OH = []
            nkb = 4 * t + 4
            for hd in range(4):
                cvb_hd = cf_conv_chunk(hd)
                oacc = []
                for m in range(2):
                    pO = PSP.alloc()
                    pZ = PSP.alloc()
                    prt = slice(m * 64, (m + 1) * 64)

                    def emitS(kb):
                        c0 = max(0, 128 * (kb - 4 * t))
                        ps = PSP.alloc()
                        MM(ps.t[:, c0:TT], KT[:, hd, kb * 128:(kb + 1) * 128], QT[hd][m].t[:, c0:TT], True, True,
                           [BKT[kb // 4], QT[hd][m].B], ps.B)
                        return ps, c0

                    SD = 2
                    pend = [emitS(kb) for kb in range(min(SD, nkb))]
                    for kb in range(nkb):
                        ps, c0 = pend.pop(0)
                        if kb + SD < nkb:
                            pend.append(emitS(kb + SD))
                        E = BP.alloc()
                        cf = 128 * max(0, kb + 2 - 4 * t)
                        near = []
                        for j in range(c0 // 128, min(4, cf // 128)):
                            dd = (4 * t + j) - kb
                            assert dd in (0, 1)
                            tmp = FP.alloc()
                            STT(tmp.t[:, 0:128], ps.t[:, j * 128:(j + 1) * 128], 0.125,
                                BT[:, hd, dd * 128:(dd + 1) * 128], ALU.mult, ALU.add, [BBT], [tmp.B, ps.B])
                            near.append((j, tmp))
                        if cf < TT:
                            ACT(E.t[:, cf:TT], ps.t[:, cf:TT], AF.Exp, [ps.B], [E.B], scale=0.125)
                        for j, tmp in near:
                            ACT(E.t[:, j * 128:(j + 1) * 128], tmp.t[:, 0:128], AF.Exp, [tmp.B], [E.B])
                            FP.release(tmp)
                        PSP.release(ps)
                        MM(pO.t[:, c0:TT], VC[:, kb, hd * 128:(hd + 1) * 128], E.t[:, c0:TT], kb == 0, kb == nkb - 1,
                           [BV[kb], E.B], pO.B)
                        MM(pZ.t[:, c0:TT], ONES[:, :], E.t[:, c0:TT], kb == 0, kb == nkb - 1, [BONES, E.B], pZ.B)
                        BP.release(E)
                    r = FP.alloc()
                    RECIP(r.t[:, :], pZ.t[:, :], [pZ.B], [r.B])
                    PSP.release(pZ)
                    o = FP.alloc()
                    TTo("dve", o.t[:, :], pO.t[:, :], r.t[:, :], ALU.mult, [pO.B, r.B], [o.B])
                    PSP.release(pO)
                    FP.release(r)
                    oacc.append(o)
                o1, o2 = oacc
                STT(o1.t[:, :], o2.t[:, :], DER[:, l, 8:9], o1.t[:, :], ALU.mult, ALU.add, [o1.B, o2.B, BDER], [o1.B])
                FP.release(o2)
                OH.append(o1)
                cf_mean_mm(hd, cvb_hd)
            for hd in range(4):
                o1 = OH[hd]
                rs = rms_rstd([(o1.t[:, :], o1.B)], 1.0 / 128)
                ya = BP.alloc()
                STT(ya.t[:, :], o1.t[:, :], DER[:, l, 9:10], rs.t[:, :], ALU.mult, ALU.mult, [o1.B, rs.B, BDER], [ya.B])
                FP.release(o1)
                FP.release(rs)
                YA.append(ya)
            for qa, qb in QT:
                BP.release(qa)
                BP.release(qb)

            if t >= 1 and STOP <= 1:
                return
            MARKS.append((l, t, "lru", P.n_mm))
            CP("pool", HC[:, :, :], CI[:, :, TT:30 + TT], BCI, [BHC])
            for c in range(4):
                STT(CV[c].t[:, :], pm.t[:, :], -1.0 / 512, CV[c].t[:, :], ALU.mult, ALU.add, [pm.B, CV[c].B], [CV[c].B])
            PSP.release(pm)
            rs = rms_rstd([(CV[c].t[:, :], CV[c].B) for c in range(4)], 1.0 / 512)
            YC = []
            for c in range(4):
                TTo("dve", CV[c].t[:, :], CV[c].t[:, :], rs.t[:, :], ALU.mult, [CV[c].B, rs.B], [CV[c].B])
                yc = BP.alloc()
                ACT(yc.t[:, :], CV[c].t[:, :], AF.Silu, [CV[c].B, BVEC], [yc.B], bias=vcol("cf_b", c),
                    scale=vcol("cf_g", c))
                FP.release(CV[c])
                YC.append(yc)
            FP.release(rs)

            if t == 0:
                P.op("pool", lambda e: e.memset(CI[:, :, 0:30], 0.0), writes=BCI)
            else:
                CP("pool", CI[:, :, 27:30], HL[:, :, :], [BHL], BCI)
            slot_x = WNEXT("lx")
            for c in range(4):
                psu = PSP.alloc()
                dense(psu, slot_x, 8, 512, c, hb)
                CP("act", CI[:, c, 30:30 + TT], psu.t[:, :], [psu.B], [BCI[c]])
                PSP.release(psu)
            WDONE(slot_x)
            slot_g = WNEXT("lg")
            YL = [None] * 4
            cw = VOFF["lru_cw"]
            for half in range(2):
                cs = (2 * half, 2 * half + 1)
                XC, RR, II, T1 = {}, {}, {}, {}
                for c in cs:
                    xc = FP.alloc()
                    TS("dve", xc.t[:, :], CI[:, c, 27:27 + TT], vcol("lru_cw", c * 4 + 0), vcol("lru_cb", c),
                       ALU.mult, ALU.add, [BCI[c], BVEC], [xc.B])
                    for k in range(1, 4):
                        STT(xc.t[:, :], CI[:, c, 27 + k:27 + k + TT], vcol("lru_cw", c * 4 + k), xc.t[:, :],
                            ALU.mult, ALU.add, [BCI[c], BVEC, xc.B], [xc.B])
                    xcb = BP.alloc()
                    CP("pool", xcb.t[:, :], xc.t[:, :], [xc.B], [xcb.B])
                    pr = PSP.alloc()
                    MM(pr.t[:, :], LW[:, l, c, :], xcb.t[:, :], True, True, [BLW, xcb.B], pr.B)
                    pi = PSP.alloc()
                    MM(pi.t[:, :], LW[:, l, 4 + c, :], xcb.t[:, :], True, True, [BLW, xcb.B], pi.B)
                    BP.release(xcb)
                    rr, ii = FP.alloc(), FP.alloc()
                    ACT(rr.t[:, :], pr.t[:, :], AF.Sigmoid, [pr.B, BVEC], [rr.B], bias=vcol("lru_ba", c))
                    ACT(ii.t[:, :], pi.t[:, :], AF.Sigmoid, [pi.B, BVEC], [ii.B], bias=vcol("lru_bx", c))
                    PSP.release(pr)
                    PSP.release(pi)
                    XC[c], RR[c], II[c] = xc, rr, ii
                for c in cs:
                    t1 = FP.alloc()
                    ACT(t1.t[:, :], RR[c].t[:, :], AF.Exp, [RR[c].B, BDER], [t1.B], scale=DER[:, l, 4 + c:5 + c])
                    ACT(RR[c].t[:, :], RR[c].t[:, :], AF.Exp, [RR[c].B, BDER], [RR[c].B], scale=DER[:, l, c:c + 1])
                    TS("dve", t1.t[:, :], t1.t[:, :], -1.0, 1.0, ALU.mult, ALU.add, [t1.B], [t1.B])
                    T1[c] = t1
                for c in cs:
                    ACT(T1[c].t[:, :], T1[c].t[:, :], AF.Sqrt, [T1[c].B], [T1[c].B])
                    TTo("dve", II[c].t[:, :], II[c].t[:, :], T1[c].t[:, :], ALU.mult, [II[c].B, T1[c].B], [II[c].B])
                    TTo("dve", II[c].t[:, :], II[c].t[:, :], XC[c].t[:, :], ALU.mult, [II[c].B, XC[c].B], [II[c].B])
                    if t == 0:
                        SCAN(XC[c].t[:, :], RR[c].t[:, :], II[c].t[:, :], 0.0, [RR[c].B, II[c].B], [XC[c].B])
                    else:
                        SCAN(XC[c].t[:, :], RR[c].t[:, :], II[c].t[:, :], HST[:, c:c + 1],
                             [RR[c].B, II[c].B, BHST[c]], [XC[c].B])
                    CP("dve", HST[:, c:c + 1], XC[c].t[:, TT - 1:TT], [XC[c].B], [BHST[c]])
                    FP.release(T1[c])
                    FP.release(RR[c])
                    pg = PSP.alloc()
                    dense(pg, slot_g, 8, 512, c, hb)
                    gl = II[c]
                    ACT(gl.t[:, :], pg.t[:, :], AF.Gelu_apprx_tanh, [pg.B], [gl.B])
                    PSP.release(pg)
                    yl = BP.alloc()
                    TTo("dve", yl.t[:, :], XC[c].t[:, :], gl.t[:, :], ALU.mult, [XC[c].B, gl.B], [yl.B])
                    FP.release(gl)
                    FP.release(XC[c])
                    YL[c] = yl
            WDONE(slot_g)
            CP("pool", HL[:, :, :], CI[:, :, 27 + TT:30 + TT], BCI, [BHL])

            if t >= 1 and STOP <= 2:
                return
            MARKS.append((l, t, "sc", P.n_mm))
            if t > 0:
                CP("pool", CI[:, :, 28:30], HS[:, :, :], [BHS], BCI)
            slot_x = WNEXT("sx")
            for c in range(4):
                psu = PSP.alloc()
                dense(psu, slot_x, 8, 512, c, hb)
                CP("act", CI[:, c, 30:30 + TT], psu.t[:, :], [psu.B], [BCI[c]])
                PSP.release(psu)
            WDONE(slot_x)
            slot_c = WNEXT("sc")
            for c in range(4):
                psu = PSP.alloc()
                dense(psu, slot_c, 8, 512, c, hb)
                TTo("dve", CI[:, c, 30:30 + TT], psu.t[:, :], CI[:, c, 30:30 + TT], ALU.mult, [psu.B, BCI[c]], [BCI[c]])
                PSP.release(psu)
            WDONE(slot_c)
            slot_b = WNEXT("sb")
            YS = []
            for c in range(4):
                acc = FP.alloc()
                TS("dve", acc.t[:, :], CI[:, c, 28:28 + TT], vcol("sc_cw", c * 3 + 0), None, ALU.mult, None,
                   [BCI[c], BVEC], [acc.B])
                for k in (1, 2):
                    STT(acc.t[:, :], CI[:, c, 28 + k:28 + k + TT], vcol("sc_cw", c * 3 + k), acc.t[:, :],
                        ALU.mult, ALU.add, [BCI[c], BVEC, acc.B], [acc.B])
                psu = PSP.alloc()
                dense(psu, slot_b, 8, 512, c, hb)
                ys = BP.alloc()
                TTo("dve", ys.t[:, :], psu.t[:, :], acc.t[:, :], ALU.mult, [psu.B, acc.B], [ys.B])
                PSP.release(psu)
                FP.release(acc)
                YS.append(ys)
            WDONE(slot_b)
            CP("pool", HS[:, :, :], CI[:, :, 28 + TT:30 + TT], BCI, [BHS])

            if debug:
                for bi, Y in enumerate((YA, YL, YS, YC)):
                    Bd = BDBG
                    P.dma("sync", [(dbg_h.ap()[l, bi, :, c, tsl], Y[c].t[:, :]) for c in range(4)],
                          reads=[Y[c].B for c in range(4)], writes=[Bd])

            if t >= 1 and STOP <= 4:
                return
            MARKS.append((l, t, "merge", P.n_mm))
            Ys = (YA, YL, YS, YC)
            MB = [None] * NCH
            for og in range(2):
                M = [FP.alloc() for _ in range(4)]
                wbr = None
                for b in range(4):
                    if b % 2 == 0:
                        wbr = WNEXT("br")
                    wg = WNEXT("gate")
                    for j in range(4):
                        oc = og * 4 + j
                        p1 = PSP.alloc()
                        for kc in range(4):
                            kk = (b % 2) * 4 + kc
                            MM(p1.t[:, :], wbr.t[:, kk * 512 + j * 128:kk * 512 + (j + 1) * 128], Ys[b][kc].t[:, :],
                               kc == 0, kc == 3, [wbr.B, Ys[b][kc].B], p1.B)
                        p2 = PSP.alloc()
                        dense(p2, wg, 8, 512, j, hb)
                        G = FP.alloc()
                        ACT(G.t[:, :], p2.t[:, :], AF.Sigmoid, [p2.B, BVEC], [G.B], bias=vcol("gate_b", b * 8 + oc))
                        PSP.release(p2)
                        if b == 0:
                            TTo("dve", M[j].t[:, :], p1.t[:, :], G.t[:, :], ALU.mult, [p1.B, G.B], [M[j].B])
                        else:
                            TTo("dve", G.t[:, :], p1.t[:, :], G.t[:, :], ALU.mult, [p1.B, G.B], [G.B])
                            if b < 3:
                                TTo("pool", M[j].t[:, :], M[j].t[:, :], G.t[:, :], ALU.add, [M[j].B, G.B], [M[j].B])
                            else:
                                mb = BP.alloc()
                                TTo("pool", mb.t[:, :], M[j].t[:, :], G.t[:, :], ALU.add, [M[j].B, G.B], [mb.B])
                                MB[oc] = mb
                        PSP.release(p1)
                        FP.release(G)
                    WDONE(wg)
                    if b % 2 == 1:
                        WDONE(wbr)
                for u in M:
                    FP.release(u)
            for Y in Ys:
                for u in Y:
                    BP.release(u)
            for u in hb:
                BP.release(u)

            if t >= 1 and STOP <= 5:
                return
            MARKS.append((l, t, "wo", P.n_mm))
            MO = []
            for og in range(2):
                wo = WNEXT("wo")
                for j in range(4):
                    psu = PSP.alloc()
                    dense(psu, wo, 8, 512, j, MB)
                    mo = FP.alloc()
                    CP("act", mo.t[:, :], psu.t[:, :], [psu.B], [mo.B])
                    PSP.release(psu)
                    MO.append(mo)
                WDONE(wo)
            for u in MB:
                BP.release(u)
            residual_add(l, MO, "g_post_mix")

            if t >= 1 and STOP <= 6:
                return
            MARKS.append((l, t, "ffn", P.n_mm))
            hb = norm_to_bf16(l, "g_pre_ffn")
            ACTS = []
            for jg in range(6):
                ncols = 512 if jg < 5 else 256
                wgt = WNEXT("fg")
                wup = WNEXT("fu")
                for j in range(ncols // 128):
                    pg = PSP.alloc()
                    dense(pg, wgt, 8, ncols, j, hb)
                    pu = PSP.alloc()
                    dense(pu, wup, 8, ncols, j, hb)
                    sg = FP.alloc()
                    ACT(sg.t[:, :], pg.t[:, :], AF.Silu, [pg.B], [sg.B])
                    PSP.release(pg)
                    a = BP.alloc()
                    TTo("dve", a.t[:, :], pu.t[:, :], sg.t[:, :], ALU.mult, [pu.B, sg.B], [a.B])
                    PSP.release(pu)
                    FP.release(sg)
                    ACTS.append(a)
                WDONE(wgt)
                WDONE(wup)
            for u in hb:
                BP.release(u)
            FO = []
            for og in range(2):
                ps4 = [PSP.alloc() for _ in range(4)]
                for kg in range(3):
                    nk = 8 if kg < 2 else 6
                    wfo = WNEXT("fo")
                    for j in range(4):
                        dense(ps4[j], wfo, nk, 512, j, ACTS, k0=kg * 8, start=(kg == 0), stop=(kg == 2))
                    WDONE(wfo)
                for j in range(4):
                    fo = FP.alloc()
                    CP("act", fo.t[:, :], ps4[j].t[:, :], [ps4[j].B], [fo.B])
                    PSP.release(ps4[j])
                    FO.append(fo)
            for u in ACTS:
                BP.release(u)
            residual_add(l, FO, "g_post_ffn")

            if t >= 1 and STOP <= 7:
                return
            MARKS.append((l, t, "ple", P.n_mm))
            pf = [FP.alloc(), FP.alloc()]
            P.dma("sync", [(pf[k].t[:, :], pT_h.ap()[l, k * 128:(k + 1) * 128, tsl]) for k in range(2)],
                  writes=[pf[0].B, pf[1].B])
            pb = []
            for k in range(2):
                u = BP.alloc()
                CP("pool", u.t[:, :], pf[k].t[:, :], [pf[k].B], [u.B])
                FP.release(pf[k])
                pb.append(u)
            hb = norm_to_bf16(l, "g_ple")
            wpi = WNEXT("pi")
            for og in range(2):
                wpg = WNEXT("pg")
                for j in range(4):
                    oc = og * 4 + j
                    pe_ = PSP.alloc()
                    for kc in range(2):
                        MM(pe_.t[:, :], wpi.t[:, kc * 1024 + oc * 128:kc * 1024 + (oc + 1) * 128], pb[kc].t[:, :],
                           kc == 0, kc == 1, [wpi.B, pb[kc].B], pe_.B)
                    pg = PSP.alloc()
                    dense(pg, wpg, 8, 512, j, hb)
                    ge = FP.alloc()
                    ACT(ge.t[:, :], pg.t[:, :], AF.Sigmoid, [pg.B], [ge.B])
                    PSP.release(pg)
                    TTo("dve", ge.t[:, :], pe_.t[:, :], ge.t[:, :], ALU.mult, [pe_.B, ge.B], [ge.B])
                    PSP.release(pe_)
                    TTo("pool", xs[oc].t[:, :], xs[oc].t[:, :], ge.t[:, :], ALU.add, [xs[oc].B, ge.B], [xs[oc].B])
                    FP.release(ge)
                WDONE(wpg)
            WDONE(wpi)
            for u in hb:
                BP.release(u)
            for u in pb:
                BP.release(u)
            if t >= 1 and STOP <= 8:
                return
            MARKS.append((l, t, "store", P.n_mm))
            P.dma("sync", [(xout_ap[c * 128:(c + 1) * 128, tsl], xs[c].t[:, :]) for c in range(NCH)],
                  reads=[u.B for u in xs], writes=[Bxout])

        Bxin0 = Buf("xin")
        BDBG = Buf("dbg")
        for l in range(NL):
            for t in range(NTE):
                if l == 0:
                    xin_ap, Bxin = xT_h.ap(), Bxin0
                else:
                    xin_ap, Bxin = xmid_h.ap(), BXMID[t]
                if l == NL - 1:
                    xout_ap, Bxout = out_h.ap(), BOUT[t]
                else:
                    xout_ap, Bxout = xmid_h.ap(), BXMID[t]
                emit_tile(l, t, xin_ap, Bxin, xout_ap, Bxout)
        P.wait_all("sync", BOUT[:NTE] + ([BDBG] if debug else []))
        info = dict(marks=MARKS, n_op=P.n_op, n_wait=P.n_wait, nsem=P.nsem, fmin=FP.minfree, bmin=BP.minfree, pmin=PSP.minfree)
        P.run()
    return nc, info


def _lam_init(l):
    return 0.8 - 0.6 * math.exp(-0.3 * l)


_PROG_CACHE = {}


def _get_prog(NL, lam_inits, debug=False):
    key = (NL, tuple(lam_inits), debug)
    if key not in _PROG_CACHE:
        _PROG_CACHE[key] = build_program(NL, lam_inits, debug)
    return _PROG_CACHE[key]


def _layer_inputs(inp, layers):
    ls = list(layers)
    f = lambda a: np.ascontiguousarray(np.asarray(a, np.float32))
    onehot, mask, J = _consts()
    d = dict(
        rel_bias=f(inp["rel_bias"]),
        c_onehot=onehot, c_mask=mask, c_J=J,
        vecs=np.stack([_pack_vecs(inp, l) for l in ls]),
        lru_bd=np.stack([_pack_lru_bd(inp, l) for l in ls]),
        w_in=f(np.asarray(inp["w_in"])[ls]),
        w_branch=f(np.asarray(inp["w_branch"])[ls]).reshape(len(ls), 2048, D),
        w_o=f(np.asarray(inp["w_o"])[ls]),
        w_ffn_in=f(np.asarray(inp["w_ffn_in"])[ls]),
        w_ffn_out=f(np.asarray(inp["w_ffn_out"])[ls]),
        w_ple_in=f(np.asarray(inp["w_ple_in"])[ls]),
        w_ple_gate=f(np.asarray(inp["w_ple_gate"])[ls]),
    )
    return d


def kernel(**inputs):
    x = np.asarray(inputs["x"], np.float32)
    p = np.asarray(inputs["p"], np.float32)
    B = x.shape[0]
    assert B == NCORES
    if FUSED:
        groups = [[0, 1]]
    else:
        groups = [[0], [1]]
    xT = [np.ascontiguousarray(x[b].T) for b in range(B)]
    for ls in groups:
        shared = _layer_inputs(inputs, ls)
        nc, _ = _get_prog(len(ls), [_lam_init(l) for l in ls])
        in_maps = []
        for b in range(B):
            m = dict(shared)
            m["xT"] = xT[b]
            m["pT"] = np.ascontiguousarray(np.stack([p[l, b].T for l in ls]))
            in_maps.append(m)
        res = run_bass_kernel_spmd(nc, in_maps, core_ids=list(range(NCORES)))
        xT = [np.asarray(res.results[b]["outT"], np.float32) for b in range(B)]
    out = np.stack([xT[b].T for b in range(B)]).astype(np.float32)
    return np.ascontiguousarray(out)
```

```python
import math
import os
from contextlib import ExitStack

import numpy as np

import concourse.bass as bass
import concourse.mybir as mybir
from concourse.bass_utils import run_bass_kernel_spmd

F32 = mybir.dt.float32
BF16 = mybir.dt.bfloat16
AF = mybir.ActivationFunctionType
ALU = mybir.AluOpType
AX = mybir.AxisListType

S = 4096
D = 1024
TT = 512
NT = S // TT
NCH = D // 128
IN_TOTAL = 9216
FFH = 2816
NJ = FFH // 128
EPS = 1e-6
NCORES = 8
FUSED = True
DEBUG = bool(int(os.environ.get("MK_DEBUG", "0")))

ENGINES = ("sync", "act", "pool", "dve", "pe")
SEM_ROT = int(os.environ.get("MK_SEMROT", 30000))


class Buf:
    __slots__ = ("name", "lastw", "reads", "dsem", "dcnt")

    def __init__(self, name):
        self.name = name
        self.lastw = {}
        self.reads = {}
        self.dsem = None
        self.dcnt = 0


class Prog:
    def __init__(self, nc, stack, strict=True):
        self.nc = nc
        self.stack = stack
        self.ops = {e: [] for e in ENGINES}
        self.esem = {}
        self.ecnt = {}
        self.seen = {e: {} for e in ENGINES}
        self.strict = strict
        self.nsem = 0
        self.n_wait = 0
        self.n_op = 0
        self.n_mm = 0
        for e in ENGINES:
            self._new_esem(e)

    def _alloc_sem(self, name):
        self.nsem += 1
        return self.stack.enter_context(self.nc.semaphore(f"{name}_{self.nsem}"))

    def _new_esem(self, e):
        self.esem[e] = self._alloc_sem("e" + e)
        self.ecnt[e] = 0

    def _deps(self, eng, reads, writes):
        need = {}
        for b in reads:
            for s, v in b.lastw.items():
                if need.get(s, 0) < v:
                    need[s] = v
        for b in writes:
            for s, v in b.lastw.items():
                if need.get(s, 0) < v:
                    need[s] = v
            for s, v in b.reads.items():
                if need.get(s, 0) < v:
                    need[s] = v
        out = []
        seen = self.seen[eng]
        own = self.esem[eng]
        for s, v in need.items():
            if s is own and (eng == "pe" or not self.strict):
                continue
            if seen.get(s, 0) >= v:
                continue
            seen[s] = v
            out.append((s, v))
        return out

    def _emit_waits(self, eng, deps):
        for s, v in deps:
            self.n_wait += 1
            self.ops[eng].append(lambda e, s=s, v=v: e.wait_ge(s, v))

    def op(self, eng, fn, reads=(), writes=()):
        deps = self._deps(eng, reads, writes)
        self._emit_waits(eng, deps)
        if self.ecnt[eng] >= SEM_ROT:
            self._new_esem(eng)
        sem = self.esem[eng]
        self.ecnt[eng] += 1
        val = self.ecnt[eng]
        self.n_op += 1
        self.ops[eng].append(lambda e, fn=fn, sem=sem: fn(e).then_inc(sem, 1))
        for b in writes:
            b.lastw = {sem: val}
            b.reads = {}
        for b in reads:
            if b not in writes:
                b.reads[sem] = val

    def dma(self, eng, pairs, reads=(), writes=()):
        deps = self._deps(eng, reads, writes)
        self._emit_waits(eng, deps)
        owner = writes[0]
        if owner.dsem is None or owner.dcnt + 16 * len(pairs) > SEM_ROT:
            owner.dsem = self._alloc_sem("d")
            owner.dcnt = 0
        sem = owner.dsem
        for (o, i) in pairs:
            owner.dcnt += 16
            self.n_op += 1
            self.ops[eng].append(lambda e, o=o, i=i, sem=sem: e.dma_start(out=o, in_=i).then_inc(sem, 16))
        val = owner.dcnt
        for b in writes:
            b.lastw = {sem: val}
            b.reads = {}
        for b in reads:
            if b not in writes:
                b.reads[sem] = val

    def wait_all(self, eng, bufs):
        self._emit_waits(eng, self._deps(eng, bufs, ()))

    def run(self):
        ops = self.ops
        with self.nc.Block() as block:
            @block.sync
            def _(e):
                for f in ops["sync"]:
                    f(e)

            @block.scalar
            def _(e):
                for f in ops["act"]:
                    f(e)

            @block.gpsimd
            def _(e):
                for f in ops["pool"]:
                    f(e)

            @block.vector
            def _(e):
                for f in ops["dve"]:
                    f(e)

            @block.tensor
            def _(e):
                for f in ops["pe"]:
                    f(e)


class Unit:
    __slots__ = ("t", "B")

    def __init__(self, t, B):
        self.t = t
        self.B = B


class UnitPool:
    def __init__(self, nc, st, name, n, shape, dtype, psum=False):
        self.free = []
        self.name = name
        for i in range(n):
            mk = nc.psum_tensor if psum else nc.sbuf_tensor
            t = st.enter_context(mk(f"{name}{i}", shape, dtype))
            self.free.append(Unit(t, Buf(f"{name}{i}")))
        self.n = n
        self.minfree = n

    def alloc(self):
        assert self.free, f"pool {self.name} exhausted"
        u = self.free.pop(0)
        self.minfree = min(self.minfree, len(self.free))
        return u

    def release(self, u):
        self.free.append(u)


VEC_LAYOUT = [("g_pre_mix", 8), ("g_post_mix", 8), ("g_pre_ffn", 8), ("g_post_ffn", 8), ("g_ple", 8),
              ("gate_b", 32), ("lru_cw", 16), ("lru_cb", 4), ("lru_ba", 4), ("lru_bx", 4), ("lru_lam", 4),
              ("sc_cw", 12), ("cf_cw", 124), ("cf_cb", 4), ("cf_g", 4), ("cf_b", 4), ("sub_g", 1),
              ("att_lam", 256)]
VOFF = {}
_c = 0
for _n, _w in VEC_LAYOUT:
    VOFF[_n] = _c
    _c += _w
NV = _c


def _fm(v):
    v = np.asarray(v, np.float32)
    return np.ascontiguousarray(v.reshape(-1, 128).T)


def _t5_bucket_np(n):
    n = np.maximum(n, 0)
    nf = np.maximum(n, 1).astype(np.float32)
    large = 16 + (np.log(nf / np.float32(16)) / np.float32(math.log(128 / 16)) * np.float32(16)).astype(np.int32)
    large = np.minimum(large, 31)
    return np.where(n < 16, n, large)


def _consts():
    d = np.arange(384) - 127
    bk = _t5_bucket_np(d)
    onehot = np.zeros((32, 384), np.float32)
    for i in range(384):
        if d[i] >= 0:
            onehot[bk[i], i] = 1.0
    mask = np.where(d < 0, -30000.0, 0.0).astype(np.float32)[None, :]
    J = np.zeros((128, 128), np.float32)
    J[np.arange(128), 127 - np.arange(128)] = 1.0
    return onehot, mask, J


def _pack_vecs(inp, l):
    V = np.zeros((128, NV), np.float32)

    def put(name, arr):
        arr = np.asarray(arr, np.float32)
        V[:, VOFF[name]:VOFF[name] + arr.shape[1]] = arr

    put("g_pre_mix", _fm(inp["g_pre_mix"][l]))
    put("g_post_mix", _fm(inp["g_post_mix"][l]))
    put("g_pre_ffn", _fm(inp["g_pre_ffn"][l]))
    put("g_post_ffn", _fm(inp["g_post_ffn"][l]))
    put("g_ple", _fm(inp["g_ple"][l]))
    put("gate_b", np.concatenate([_fm(inp["gate_b"][l, b]) for b in range(4)], axis=1))
    def convw(w):
        K = w.shape[0]
        w = np.asarray(w, np.float32).reshape(K, -1, 128)
        return np.ascontiguousarray(w.transpose(2, 1, 0).reshape(128, -1))
    put("lru_cw", convw(inp["lru_conv_w"][l]))
    put("lru_cb", _fm(inp["lru_conv_b"][l]))
    put("lru_ba", _fm(np.asarray(inp["lru_ba"][l]).reshape(-1)))
    put("lru_bx", _fm(np.asarray(inp["lru_bx"][l]).reshape(-1)))
    put("lru_lam", _fm(inp["lru_lambda"][l]))
    put("sc_cw", convw(inp["sc_conv_w"][l]))
    put("cf_cw", convw(inp["cf_conv_w"][l]))
    put("cf_cb", _fm(inp["cf_conv_b"][l]))
    put("cf_g", _fm(inp["cf_ln_g"][l]))
    put("cf_b", _fm(inp["cf_ln_b"][l]))
    put("sub_g", np.asarray(inp["att_subnorm_g"][l], np.float32).reshape(128, 1))
    put("att_lam", np.broadcast_to(np.asarray(inp["att_lambda"][l], np.float32).reshape(1, 256), (128, 256)))
    return V


def _pack_lru_bd(inp, l):
    out = np.zeros((2, 4, 128, 128), np.float32)
    for g, name in enumerate(("lru_wa", "lru_wx")):
        w = np.asarray(inp[name][l], np.float32)
        for c in range(4):
            out[g, c, 0:64, 0:64] = w[2 * c]
            out[g, c, 64:128, 64:128] = w[2 * c + 1]
    return out


def build_program(NL, lam_inits, debug=False):
    nc = bass.Bass("TRN2", target_bir_lowering=False)

    def din(name, shape, dt=F32):
        return nc.dram_tensor(name, list(shape), dt, kind="ExternalInput")

    xT_h = din("xT", [D, S])
    pT_h = din("pT", [NL, 256, S])
    relb_h = din("rel_bias", [32, 4])
    onehot_h = din("c_onehot", [32, 384])
    mask_h = din("c_mask", [1, 384])
    J_h = din("c_J", [128, 128])
    vecs_h = din("vecs", [NL, 128, NV])
    lrubd_h = din("lru_bd", [NL, 2, 4, 128, 128])
    w_in_h = din("w_in", [NL, D, IN_TOTAL])
    w_br_h = din("w_branch", [NL, 2048, D])
    w_o_h = din("w_o", [NL, D, D])
    w_fi_h = din("w_ffn_in", [NL, D, 2 * FFH])
    w_fo_h = din("w_ffn_out", [NL, FFH, D])
    w_pi_h = din("w_ple_in", [NL, 256, D])
    w_pg_h = din("w_ple_gate", [NL, D, D])
    out_h = nc.dram_tensor("outT", [D, S], F32, kind="ExternalOutput")

    def dint(name, shape, dt):
        return nc.dram_tensor(name, list(shape), dt, kind="Internal")

    xmid_h = dint("xmid", [D, S], F32) if NL > 1 else None
    vecD_h = dint("vecD", [4, 384], F32)
    NG = 45
    wsc = {l: dint(f"wsc{l}", [NG, 128, 4096], BF16) for l in range(NL)}
    wsrc = {l: dict(w_in=w_in_h.ap()[l], w_br=w_br_h.ap()[l], w_o=w_o_h.ap()[l], w_fi=w_fi_h.ap()[l],
                    w_fo=w_fo_h.ap()[l], w_pi=w_pi_h.ap()[l], w_pg=w_pg_h.ap()[l]) for l in range(NL)}
    dbg_h = None
    if debug:
        dbg_h = nc.dram_tensor("dbg", [NL, 4, 128, 4, S], BF16, kind="ExternalOutput")

    with ExitStack() as st:
        P = Prog(nc, st, strict=True)

        def sb(name, shape, dt):
            return st.enter_context(nc.sbuf_tensor(name, list(shape), dt))

        KT = sb("KT", [128, 4, S], BF16)
        VC = sb("VC", [128, S // 128, 512], BF16)
        BKT = [Buf(f"KT{t}") for t in range(NT)]
        BV = [Buf(f"V{b}") for b in range(S // 128)]
        CI = sb("CI", [128, 4, 30 + TT], F32)
        BCI = [Buf(f"CI{c}") for c in range(4)]
        HL = sb("HL", [128, 4, 3], F32)
        HS = sb("HS", [128, 4, 2], F32)
        HC = sb("HC", [128, 4, 30], F32)
        BHL, BHS, BHC = Buf("HL"), Buf("HS"), Buf("HC")
        HST = sb("HST", [128, 4], F32)
        BHST = [Buf(f"HST{c}") for c in range(4)]
        VEC = sb("VEC", [128, NL, NV], F32)
        BVEC = Buf("VEC")
        DER = sb("DER", [128, NL, 16], F32)
        BDER = Buf("DER")
        CST = sb("CST", [128, 4], F32)
        BCST = Buf("CST")
        ONES = sb("ONES", [128, 128], BF16)
        BONES = Buf("ONES")
        BT = sb("BT", [128, 4, 256], F32)
        BBT = Buf("BT")
        BFAR = sb("BFAR", [128, 4], F32)
        BBFAR = Buf("BFAR")
        LW = sb("LW", [128, NL, 8, 128], BF16)
        BLW = Buf("LW")
        NSLOT = 4
        SLOTS = [Unit(sb(f"slot{i}", [128, 4096], BF16), Buf(f"slot{i}")) for i in range(NSLOT)]

        FP = UnitPool(nc, st, "F", 22, [128, TT], F32)
        BP = UnitPool(nc, st, "B", 40, [128, TT], BF16)
        PSP = UnitPool(nc, st, "PS", 8, [128, TT], F32, psum=True)

        def MM(ps_ap, lhsT, rhs, start, stop, reads, psB):
            P.n_mm += 1
            P.op("pe", lambda e: e.matmul(ps_ap, lhsT=lhsT, rhs=rhs, start=start, stop=stop), reads=reads, writes=[psB])

        def ACT(out, in_, func, reads, writes, bias=None, scale=None):
            kw = {}
            if bias is not None:
                kw["bias"] = bias
            if scale is not None:
                kw["scale"] = scale
            P.op("act", lambda e: e.activation(out=out, in_=in_, func=func, **kw), reads=reads, writes=writes)

        def TTo(eng, out, in0, in1, op, reads, writes):
            P.op(eng, lambda e: e.tensor_tensor(out=out, in0=in0, in1=in1, op=op), reads=reads, writes=writes)

        def TS(eng, out, in0, s1, s2, op0, op1, reads, writes):
            if op1 is None:
                P.op(eng, lambda e: e.tensor_scalar(out=out, in0=in0, scalar1=s1, scalar2=None, op0=op0),
                     reads=reads, writes=writes)
            else:
                P.op(eng, lambda e: e.tensor_scalar(out=out, in0=in0, scalar1=s1, scalar2=s2, op0=op0, op1=op1),
                     reads=reads, writes=writes)

        def STT(out, in0, scalar, in1, op0, op1, reads, writes):
            P.op("dve", lambda e: e.scalar_tensor_tensor(out=out, in0=in0, scalar=scalar, in1=in1, op0=op0, op1=op1),
                 reads=reads, writes=writes)

        def CP(eng, out, in_, reads, writes):
            if eng == "act":
                P.op("act", lambda e: e.activation(out=out, in_=in_, func=AF.Copy), reads=reads, writes=writes)
            else:
                P.op(eng, lambda e: e.tensor_copy(out=out, in_=in_), reads=reads, writes=writes)

        def SCAN(out, d0, d1, init, reads, writes):
            P.op("dve", lambda e: e.tensor_tensor_scan(out=out, data0=d0, data1=d1, initial=init,
                                                       op0=ALU.mult, op1=ALU.add), reads=reads, writes=writes)

        def RECIP(out, in_, reads, writes):
            P.op("dve", lambda e: e.reciprocal(out=out, in_=in_), reads=reads, writes=writes)

        GSETS = [3, 2, 2, 3, 6, 6, 2, 12, 6, 3]
        assert sum(GSETS) == NG
        BWG = {}
        for l in range(NL):
            BWG[l] = []
            for si, n in enumerate(GSETS):
                b = Buf(f"wg{l}_{si}")
                BWG[l] += [b] * n

        def group_list(l):
            W = wsrc[l]
            sc = []
            win = lambda tag, chunk0: (tag, W["w_in"], 0, 8, chunk0 * 128, 512)
            sc += [win("q", 0), win("k", 4), win("v", 8), win("cg", 36), win("ca", 32)]
            sc += [win("lx", 12), win("lg", 16), win("sx", 28), win("sc", 24), win("sb", 20)]
            for og in range(2):
                for b in range(4):
                    if b % 2 == 0:
                        sc.append(("br", W["w_br"], (b // 2) * 1024, 8, og * 512, 512))
                    sc.append(win("gate", 40 + b * 8 + og * 4))
            for og in range(2):
                sc.append(("wo", W["w_o"], 0, 8, og * 512, 512))
            for jg in range(6):
                ncols = 512 if jg < 5 else 256
                sc.append(("fg", W["w_fi"], 0, 8, jg * 512, ncols))
                sc.append(("fu", W["w_fi"], 0, 8, FFH + jg * 512, ncols))
            for og in range(2):
                for kg in range(3):
                    nk = 8 if kg < 2 else 6
                    sc.append(("fo", W["w_fo"], kg * 1024, nk, og * 512, 512))
            sc.append(("pi", W["w_pi"], 0, 2, 0, 1024))
            for og in range(2):
                sc.append(("pg", W["w_pg"], 0, 8, og * 512, 512))
            assert len(sc) == NG
            return sc

        GL = {l: group_list(l) for l in range(NL)}

        def convert_layer(l):
            g = 0
            for n in GSETS:
                pairs = []
                for _ in range(n):
                    tag, src, r0, nk, c0, ncols = GL[l][g]
                    s_ap = src[r0:r0 + nk * 128, c0:c0 + ncols].rearrange("(k p) n -> p k n", p=128)
                    d_ap = wsc[l].ap()[g, :, 0:nk * ncols].rearrange("p (k n) -> p k n", k=nk)
                    pairs.append((d_ap, s_ap))
                    g += 1
                P.dma("pool", pairs, writes=[BWG[l][g - 1]])

        P.dma("sync", [(VEC[:, l, :], vecs_h.ap()[l]) for l in range(NL)], writes=[BVEC])
        P.op("dve", lambda e: e.memset(CST[:, 0:1], EPS), writes=[BCST])
        P.op("dve", lambda e: e.memset(ONES[:, :], 1.0), writes=[BONES])
        P.dma("pool", [(LW[:, l, :, :], lrubd_h.ap()[l].rearrange("g c i j -> i (g c) j")) for l in range(NL)],
              writes=[BLW])
        convert_layer(0)
        if NL > 1:
            convert_layer(1)

        u1, u2, u3 = FP.alloc(), FP.alloc(), FP.alloc()
        P.dma("sync", [(u1.t[0:32, 0:4], relb_h.ap()), (u1.t[0:32, 8:392], onehot_h.ap())], writes=[u1.B])
        P.dma("sync", [(u2.t[0:1, 8:392], mask_h.ap()), (u3.t[:, 0:128], J_h.ap())], writes=[u2.B, u3.B])
        P.op("dve", lambda e: e.memset(u2.t[0:1, 0:4], 1.0), reads=[], writes=[u2.B])
        psb = PSP.alloc()
        MM(psb.t[0:4, 0:384], u1.t[0:32, 0:4], u1.t[0:32, 8:392], True, False, [u1.B], psb.B)
        MM(psb.t[0:4, 0:384], u2.t[0:1, 0:4], u2.t[0:1, 8:392], False, True, [u2.B], psb.B)
        u4 = FP.alloc()
        CP("dve", u4.t[0:4, 0:384], psb.t[0:4, 0:384], [psb.B], [u4.B])
        PSP.release(psb)
        BvecD = Buf("vecD")
        P.dma("sync", [(vecD_h.ap(), u4.t[0:4, 0:384])], reads=[u4.B], writes=[BvecD])
        for h in range(4):
            hk = FP.alloc()
            P.dma("sync", [(hk.t[:, 0:256], bass.AP(vecD_h, h * 384, [[1, 128], [1, 256]]))], reads=[BvecD],
                  writes=[hk.B])
            psb = PSP.alloc()
            MM(psb.t[:, 0:256], u3.t[:, 0:128], hk.t[:, 0:256], True, True, [u3.B, hk.B], psb.B)
            CP("dve", BT[:, h, :], psb.t[:, 0:256], [psb.B], [BBT])
            PSP.release(psb)
            FP.release(hk)
            CP("dve", BFAR[:, h:h + 1], BT[:, h, 255:256], [BBT], [BBFAR])
            TS("dve", BT[:, h, :], BT[:, h, :], BFAR[:, h:h + 1], None, ALU.subtract, None, [BBT, BBFAR], [BBT])
        for u in (u1, u2, u3, u4):
            FP.release(u)

        for l in range(NL):
            vo = lambda name, i=0, l=l: VEC[:, l, VOFF[name] + i:VOFF[name] + i + 1]
            tmp = FP.alloc()
            la = VOFF["lru_lam"]
            ACT(tmp.t[:, 0:4], VEC[:, l, la:la + 4], AF.Exp, [BVEC], [tmp.B], scale=-1.0)
            TS("dve", tmp.t[:, 0:4], tmp.t[:, 0:4], 1.0, None, ALU.add, None, [tmp.B], [tmp.B])
            ACT(tmp.t[:, 4:8], tmp.t[:, 0:4], AF.Ln, [tmp.B], [tmp.B])
            TS("dve", DER[:, l, 0:4], tmp.t[:, 4:8], -8.0, None, ALU.mult, None, [tmp.B], [BDER])
            TS("dve", DER[:, l, 4:8], tmp.t[:, 4:8], -16.0, None, ALU.mult, None, [tmp.B], [BDER])
            al = VOFF["att_lam"]
            TTo("dve", tmp.t[:, 16:80], VEC[:, l, al:al + 64], VEC[:, l, al + 64:al + 128], ALU.mult, [BVEC], [tmp.B])
            TTo("dve", tmp.t[:, 80:144], VEC[:, l, al + 128:al + 192], VEC[:, l, al + 192:al + 256], ALU.mult,
                [BVEC], [tmp.B])
            P.op("dve", lambda e, tmp=tmp: e.reduce_sum(out=tmp.t[:, 8:9], in_=tmp.t[:, 16:80], axis=AX.X),
                 reads=[tmp.B], writes=[tmp.B])
            P.op("dve", lambda e, tmp=tmp: e.reduce_sum(out=tmp.t[:, 9:10], in_=tmp.t[:, 80:144], axis=AX.X),
                 reads=[tmp.B], writes=[tmp.B])
            ACT(tmp.t[:, 10:12], tmp.t[:, 8:10], AF.Exp, [tmp.B], [tmp.B])
            TTo("dve", tmp.t[:, 12:13], tmp.t[:, 11:12], tmp.t[:, 10:11], ALU.subtract, [tmp.B], [tmp.B])
            TS("dve", DER[:, l, 8:9], tmp.t[:, 12:13], -float(lam_inits[l]), None, ALU.add, None, [tmp.B], [BDER])
            TS("dve", DER[:, l, 9:10], vo("sub_g"), 1.0 - float(lam_inits[l]), None, ALU.mult, None, [BVEC], [BDER])
            FP.release(tmp)

        class WStream:
            def __init__(self):
                self.sched = []
                self.pos = 0
                self.issued = 0
                self.busy = [False] * NSLOT

            def _issue(self, i):
                tag, l, g, n = self.sched[i]
                slot = SLOTS[i % NSLOT]
                P.dma("sync", [(slot.t[:, 0:n], wsc[l].ap()[g, :, 0:n])], reads=[BWG[l][g]], writes=[slot.B])

            def pump(self):
                while self.issued < len(self.sched) and not self.busy[self.issued % NSLOT]:
                    self._issue(self.issued)
                    self.busy[self.issued % NSLOT] = True
                    self.issued += 1

            def next(self, tag):
                self.pump()
                assert self.issued > self.pos, "weight slots deadlock"
                ent = self.sched[self.pos]
                assert ent[0] == tag, (self.pos, ent[0], tag)
                slot = SLOTS[self.pos % NSLOT]
                self.pos += 1
                return slot

            def done(self, slot):
                i = SLOTS.index(slot)
                assert self.busy[i]
                self.busy[i] = False
                self.pump()

        WS = WStream()

        def sched_tile(l):
            return [(tag, l, g, nk * ncols) for g, (tag, src, r0, nk, c0, ncols) in enumerate(GL[l])]

        NTE = int(os.environ.get("MK_NTILES", NT))
        for l in range(NL):
            for t in range(NTE):
                WS.sched.extend(sched_tile(l))
        def WNEXT(tag):
            return WS.next(tag)

        WDONE = WS.done

        xs = [FP.alloc() for _ in range(NCH)]
        BXMID = [Buf(f"xmid{t}") for t in range(NT)]
        BOUT = [Buf(f"out{t}") for t in range(NT)]

        def rms_rstd(srcs, inv_n):
            psu = PSP.alloc()
            n = len(srcs)
            for c, (ap, b) in enumerate(srcs):
                sq = BP.alloc()
                ACT(sq.t[:, :], ap, AF.Square, [b], [sq.B])
                MM(psu.t[:, :], ONES[:, :], sq.t[:, :], c == 0, c == n - 1, [sq.B, BONES], psu.B)
                BP.release(sq)
            r = FP.alloc()
            ACT(r.t[:, :], psu.t[:, :], AF.Sqrt, [psu.B, BCST], [r.B], bias=CST[:, 0:1], scale=inv_n)
            PSP.release(psu)
            RECIP(r.t[:, :], r.t[:, :], [r.B], [r.B])
            return r

        def norm_to_bf16(l, gname):
            r = rms_rstd([(xs[c].t[:, :], xs[c].B) for c in range(NCH)], 1.0 / D)
            hb = []
            for c in range(NCH):
                u = BP.alloc()
                STT(u.t[:, :], xs[c].t[:, :], VEC[:, l, VOFF[gname] + c:VOFF[gname] + c + 1], r.t[:, :],
                    ALU.mult, ALU.mult, [xs[c].B, r.B, BVEC], [u.B])
                hb.append(u)
            FP.release(r)
            return hb

        def dense(psu, slot, nk, ncols, j, rhs, k0=0, start=True, stop=True):
            for kc in range(nk):
                MM(psu.t[:, :], slot.t[:, kc * ncols + j * 128:kc * ncols + (j + 1) * 128], rhs[k0 + kc].t[:, :],
                   start and kc == 0, stop and kc == nk - 1, [slot.B, rhs[k0 + kc].B], psu.B)

        def residual_add(l, srcs, gname):
            r = rms_rstd([(u.t[:, :], u.B) for u in srcs], 1.0 / D)
            for c in range(NCH):
                u = srcs[c]
                STT(u.t[:, :], u.t[:, :], VEC[:, l, VOFF[gname] + c:VOFF[gname] + c + 1], r.t[:, :],
                    ALU.mult, ALU.mult, [u.B, r.B, BVEC], [u.B])
                TTo("pool" if c in (1, 4, 7) else "dve", xs[c].t[:, :], xs[c].t[:, :], u.t[:, :], ALU.add,
                    [xs[c].B, u.B], [xs[c].B])
                FP.release(u)
            FP.release(r)

        STOP = int(os.environ.get("MK_STOP", 99))
        MARKS = []
        pe_total = lambda: sum(1 for f in P.ops["pe"])

        def emit_tile(l, t, xin_ap, Bxin, xout_ap, Bxout):
            tsl = slice(t * TT, (t + 1) * TT)
            vcol = lambda name, i: VEC[:, l, VOFF[name] + i:VOFF[name] + i + 1]
            MARKS.append((l, t, "start", P.n_mm))
            P.dma("sync", [(xs[c].t[:, :], xin_ap[c * 128:(c + 1) * 128, tsl]) for c in range(NCH)],
                  reads=[Bxin], writes=[u.B for u in xs])
            hb = norm_to_bf16(l, "g_pre_mix")
            slot = WNEXT("q")
            QT = []
            for hd in range(4):
                psu = PSP.alloc()
                dense(psu, slot, 8, 512, hd, hb)
                qa, qb = BP.alloc(), BP.alloc()
                P.op("pool", lambda e, qa=qa: e.memset(qa.t[64:128, :], 0.0), writes=[qa.B])
                P.op("pool", lambda e, qb=qb: e.memset(qb.t[0:64, :], 0.0), writes=[qb.B])
                CP("act", qa.t[0:64, :], psu.t[0:64, :], [psu.B], [qa.B])
                CP("act", qb.t[64:128, :], psu.t[64:128, :], [psu.B], [qb.B])
                PSP.release(psu)
                QT.append((qa, qb))
            WDONE(slot)
            slot = WNEXT("k")
            for hd in range(4):
                psu = PSP.alloc()
                dense(psu, slot, 8, 512, hd, hb)
                CP("act" if hd % 2 == 1 else "dve", KT[:, hd, tsl], psu.t[:, :], [psu.B], [BKT[t]])
                PSP.release(psu)
            WDONE(slot)
            slot = WNEXT("v")
            for j in range(4):
                psu = PSP.alloc()
                for kc in range(8):
                    MM(psu.t[:, :], hb[kc].t[:, j * 128:(j + 1) * 128], slot.t[:, kc * 512:(kc + 1) * 512],
                       kc == 0, kc == 7, [slot.B, hb[kc].B], psu.B)
                CP("act" if j % 2 == 0 else "dve", VC[:, t * 4 + j, :], psu.t[:, :], [psu.B], [BV[t * 4 + j]])
                PSP.release(psu)
            WDONE(slot)

            if t >= 1 and STOP <= 0:
                return
            MARKS.append((l, t, "attn", P.n_mm))
            if t == 0:
                P.op("pool", lambda e: e.memset(CI[:, :, 0:30], 0.0), writes=BCI)
            else:
                CP("pool", CI[:, :, 0:30], HC[:, :, :], [BHC], BCI)
            slot_g = WNEXT("cg")
            for c in range(4):
                psu = PSP.alloc()
                dense(psu, slot_g, 8, 512, c, hb)
                ACT(CI[:, c, 30:30 + TT], psu.t[:, :], AF.Sigmoid, [psu.B], [BCI[c]])
                PSP.release(psu)
            WDONE(slot_g)
            slot_a = WNEXT("ca")
            for c in range(4):
                psu = PSP.alloc()
                dense(psu, slot_a, 8, 512, c, hb)
                TTo("dve", CI[:, c, 30:30 + TT], psu.t[:, :], CI[:, c, 30:30 + TT], ALU.mult, [psu.B, BCI[c]], [BCI[c]])
                PSP.release(psu)
            WDONE(slot_a)
            CV = []
            pm = PSP.alloc()

            def cf_conv_chunk(c):
                cv = FP.alloc()
                TS("dve", cv.t[:, :], CI[:, c, 0:TT], vcol("cf_cw", c * 31 + 0), vcol("cf_cb", c), ALU.mult, ALU.add,
                   [BCI[c], BVEC], [cv.B])
                for k in range(1, 31):
                    STT(cv.t[:, :], CI[:, c, k:k + TT], vcol("cf_cw", c * 31 + k), cv.t[:, :], ALU.mult, ALU.add,
                        [BCI[c], BVEC, cv.B], [cv.B])
                cvb = BP.alloc()
                CP("pool", cvb.t[:, :], cv.t[:, :], [cv.B], [cvb.B])
                CV.append(cv)
                return cvb

            def cf_mean_mm(c, cvb):
                MM(pm.t[:, :], ONES[:, :], cvb.t[:, :], c == 0, c == 3, [BONES, cvb.B], pm.B)
                BP.release(cvb)

            YA = []
            OH = []
            nkb = 4 * t + 4
            for hd in range(4):
                cvb_hd = cf_conv_chunk(hd)
                oacc = []
                for m in range(2):
                    pO = PSP.alloc()
                    pZ = PSP.alloc()
                    prt = slice(m * 64, (m + 1) * 64)

                    def emitS(kb):
                        c0 = max(0, 128 * (kb - 4 * t))
                        ps = PSP.alloc()
                        MM(ps.t[:, c0:TT], KT[:, hd, kb * 128:(kb + 1) * 128], QT[hd][m].t[:, c0:TT], True, True,
                           [BKT[kb // 4], QT[hd][m].B], ps.B)
                        return ps, c0

                    SD = 2
                    pend = [emitS(kb) for kb in range(min(SD, nkb))]
                    for kb in range(nkb):
                        ps, c0 = pend.pop(0)
                        if kb + SD < nkb:
                            pend.append(emitS(kb + SD))
                        E = BP.alloc()
                        cf = 128 * max(0, kb + 2 - 4 * t)
                        near = []
                        for j in range(c0 // 128, min(4, cf // 128)):
                            dd = (4 * t + j) - kb
                            assert dd in (0, 1)
                            tmp = FP.alloc()
                            STT(tmp.t[:, 0:128], ps.t[:, j * 128:(j + 1) * 128], 0.125,
                                BT[:, hd, dd * 128:(dd + 1) * 128], ALU.mult, ALU.add, [BBT], [tmp.B, ps.B])
                            near.append((j, tmp))
                        if cf < TT:
                            ACT(E.t[:, cf:TT], ps.t[:, cf:TT], AF.Exp, [ps.B], [E.B], scale=0.125)
                        for j, tmp in near:
                            ACT(E.t[:, j * 128:(j + 1) * 128], tmp.t[:, 0:128], AF.Exp, [tmp.B], [E.B])
                            FP.release(tmp)
                        PSP.release(ps)
                        MM(pO.t[:, c0:TT], VC[:, kb, hd * 128:(hd + 1) * 128], E.t[:, c0:TT], kb == 0, kb == nkb - 1,
                           [BV[kb], E.B], pO.B)
                        MM(pZ.t[:, c0:TT], ONES[:, :], E.t[:, c0:TT], kb == 0, kb == nkb - 1, [BONES, E.B], pZ.B)
                        BP.release(E)
                    r = FP.alloc()
                    RECIP(r.t[:, :], pZ.t[:, :], [pZ.B], [r.B])
                    PSP.release(pZ)
                    o = FP.alloc()
                    TTo("dve", o.t[:, :], pO.t[:, :], r.t[:, :], ALU.mult, [pO.B, r.B], [o.B])
                    PSP.release(pO)
                    FP.release(r)
                    oacc.append(o)
                o1, o2 = oacc
                STT(o1.t[:, :], o2.t[:, :], DER[:, l, 8:9], o1.t[:, :], ALU.mult, ALU.add, [o1.B, o2.B, BDER], [o1.B])
                FP.release(o2)
                OH.append(o1)
                cf_mean_mm(hd, cvb_hd)
            for hd in range(4):
                o1 = OH[hd]
                rs = rms_rstd([(o1.t[:, :], o1.B)], 1.0 / 128)
                ya = BP.alloc()
                STT(ya.t[:, :], o1.t[:, :], DER[:, l, 9:10], rs.t[:, :], ALU.mult, ALU.mult, [o1.B, rs.B, BDER], [ya.B])
                FP.release(o1)
                FP.release(rs)
                YA.append(ya)
            for qa, qb in QT:
                BP.release(qa)
                BP.release(qb)

            if t >= 1 and STOP <= 1:
                return
            MARKS.append((l, t, "lru", P.n_mm))
            CP("pool", HC[:, :, :], CI[:, :, TT:30 + TT], BCI, [BHC])
            for c in range(4):
                STT(CV[c].t[:, :], pm.t[:, :], -1.0 / 512, CV[c].t[:, :], ALU.mult, ALU.add, [pm.B, CV[c].B], [CV[c].B])
            PSP.release(pm)
            rs = rms_rstd([(CV[c].t[:, :], CV[c].B) for c in range(4)], 1.0 / 512)
            YC = []
            for c in range(4):
                TTo("dve", CV[c].t[:, :], CV[c].t[:, :], rs.t[:, :], ALU.mult, [CV[c].B, rs.B], [CV[c].B])
                yc = BP.alloc()
                ACT(yc.t[:, :], CV[c].t[:, :], AF.Silu, [CV[c].B, BVEC], [yc.B], bias=vcol("cf_b", c),
                    scale=vcol("cf_g", c))
                FP.release(CV[c])
                YC.append(yc)
            FP.release(rs)

            if t == 0:
                P.op("pool", lambda e: e.memset(CI[:, :, 0:30], 0.0), writes=BCI)
            else:
                CP("pool", CI[:, :, 27:30], HL[:, :, :], [BHL], BCI)
            slot_x = WNEXT("lx")
            for c in range(4):
                psu = PSP.alloc()
                dense(psu, slot_x, 8, 512, c, hb)
                CP("act", CI[:, c, 30:30 + TT], psu.t[:, :], [psu.B], [BCI[c]])
                PSP.release(psu)
            WDONE(slot_x)
            slot_g = WNEXT("lg")
            YL = [None] * 4
            cw = VOFF["lru_cw"]
            for half in range(2):
                cs = (2 * half, 2 * half + 1)
                XC, RR, II, T1 = {}, {}, {}, {}
                for c in cs:
                    xc = FP.alloc()
                    TS("dve", xc.t[:, :], CI[:, c, 27:27 + TT], vcol("lru_cw", c * 4 + 0), vcol("lru_cb", c),
                       ALU.mult, ALU.add, [BCI[c], BVEC], [xc.B])
                    for k in range(1, 4):
                        STT(xc.t[:, :], CI[:, c, 27 + k:27 + k + TT], vcol("lru_cw", c * 4 + k), xc.t[:, :],
                            ALU.mult, ALU.add, [BCI[c], BVEC, xc.B], [xc.B])
                    xcb = BP.alloc()
                    CP("pool", xcb.t[:, :], xc.t[:, :], [xc.B], [xcb.B])
                    pr = PSP.alloc()
                    MM(pr.t[:, :], LW[:, l, c, :], xcb.t[:, :], True, True, [BLW, xcb.B], pr.B)
                    pi = PSP.alloc()
                    MM(pi.t[:, :], LW[:, l, 4 + c, :], xcb.t[:, :], True, True, [BLW, xcb.B], pi.B)
                    BP.release(xcb)
                    rr, ii = FP.alloc(), FP.alloc()
                    ACT(rr.t[:, :], pr.t[:, :], AF.Sigmoid, [pr.B, BVEC], [rr.B], bias=vcol("lru_ba", c))
                    ACT(ii.t[:, :], pi.t[:, :], AF.Sigmoid, [pi.B, BVEC], [ii.B], bias=vcol("lru_bx", c))
                    PSP.release(pr)
                    PSP.release(pi)
                    XC[c], RR[c], II[c] = xc, rr, ii
                for c in cs:
                    t1 = FP.alloc()
                    ACT(t1.t[:, :], RR[c].t[:, :], AF.Exp, [RR[c].B, BDER], [t1.B], scale=DER[:, l, 4 + c:5 + c])
                    ACT(RR[c].t[:, :], RR[c].t[:, :], AF.Exp, [RR[c].B, BDER], [RR[c].B], scale=DER[:, l, c:c + 1])
                    TS("dve", t1.t[:, :], t1.t[:, :], -1.0, 1.0, ALU.mult, ALU.add, [t1.B], [t1.B])
                    T1[c] = t1
                for c in cs:
                    ACT(T1[c].t[:, :], T1[c].t[:, :], AF.Sqrt, [T1[c].B], [T1[c].B])
                    TTo("dve", II[c].t[:, :], II[c].t[:, :], T1[c].t[:, :], ALU.mult, [II[c].B, T1[c].B], [II[c].B])
                    TTo("dve", II[c].t[:, :], II[c].t[:, :], XC[c].t[:, :], ALU.mult, [II[c].B, XC[c].B], [II[c].B])
                    if t == 0:
                        SCAN(XC[c].t[:, :], RR[c].t[:, :], II[c].t[:, :], 0.0, [RR[c].B, II[c].B], [XC[c].B])
                    else:
                        SCAN(XC[c].t[:, :], RR[c].t[:, :], II[c].t[:, :], HST[:, c:c + 1],
                             [RR[c].B, II[c].B, BHST[c]], [XC[c].B])
                    CP("dve", HST[:, c:c + 1], XC[c].t[:, TT - 1:TT], [XC[c].B], [BHST[c]])
                    FP.release(T1[c])
                    FP.release(RR[c])
                    pg = PSP.alloc()
                    dense(pg, slot_g, 8, 512, c, hb)
                    gl = II[c]
                    ACT(gl.t[:, :], pg.t[:, :], AF.Gelu_apprx_tanh, [pg.B], [gl.B])
                    PSP.release(pg)
                    yl = BP.alloc()
                    TTo("dve", yl.t[:, :], XC[c].t[:, :], gl.t[:, :], ALU.mult, [XC[c].B, gl.B], [yl.B])
                    FP.release(gl)
                    FP.release(XC[c])
                    YL[c] = yl
            WDONE(slot_g)
            CP("pool", HL[:, :, :], CI[:, :, 27 + TT:30 + TT], BCI, [BHL])

            if t >= 1 and STOP <= 2:
                return
            MARKS.append((l, t, "sc", P.n_mm))
            if t > 0:
                CP("pool", CI[:, :, 28:30], HS[:, :, :], [BHS], BCI)
            slot_x = WNEXT("sx")
            for c in range(4):
                psu = PSP.alloc()
                dense(psu, slot_x, 8, 512, c, hb)
                CP("act", CI[:, c, 30:30 + TT], psu.t[:, :], [psu.B], [BCI[c]])
                PSP.release(psu)
            WDONE(slot_x)
            slot_c = WNEXT("sc")
            for c in range(4):
                psu = PSP.alloc()
                dense(psu, slot_c, 8, 512, c, hb)
                TTo("dve", CI[:, c, 30:30 + TT], psu.t[:, :], CI[:, c, 30:30 + TT], ALU.mult, [psu.B, BCI[c]], [BCI[c]])
                PSP.release(psu)
            WDONE(slot_c)
            slot_b = WNEXT("sb")
            YS = []
            for c in range(4):
                acc = FP.alloc()
                TS("dve", acc.t[:, :], CI[:, c, 28:28 + TT], vcol("sc_cw", c * 3 + 0), None, ALU.mult, None,
                   [BCI[c], BVEC], [acc.B])
                for k in (1, 2):
                    STT(acc.t[:, :], CI[:, c, 28 + k:28 + k + TT], vcol("sc_cw", c * 3 + k), acc.t[:, :],
                        ALU.mult, ALU.add, [BCI[c], BVEC, acc.B], [acc.B])
                psu = PSP.alloc()
                dense(psu, slot_b, 8, 512, c, hb)
                ys = BP.alloc()
                TTo("dve", ys.t[:, :], psu.t[:, :], acc.t[:, :], ALU.mult, [psu.B, acc.B], [ys.B])
                PSP.release(psu)
                FP.release(acc)
                YS.append(ys)
            WDONE(slot_b)
            CP("pool", HS[:, :, :], CI[:, :, 28 + TT:30 + TT], BCI, [BHS])

            if debug:
                for bi, Y in enumerate((YA, YL, YS, YC)):
                    Bd = BDBG
                    P.dma("sync", [(dbg_h.ap()[l, bi, :, c, tsl], Y[c].t[:, :]) for c in range(4)],
                          reads=[Y[c].B for c in range(4)], writes=[Bd])

            if t >= 1 and STOP <= 4:
                return
            MARKS.append((l, t, "merge", P.n_mm))
            Ys = (YA, YL, YS, YC)
            MB = [None] * NCH
            for og in range(2):
                M = [FP.alloc() for _ in range(4)]
                wbr = None
                for b in range(4):
                    if b % 2 == 0:
                        wbr = WNEXT("br")
                    wg = WNEXT("gate")
                    for j in range(4):
                        oc = og * 4 + j
                        p1 = PSP.alloc()
                        for kc in range(4):
                            kk = (b % 2) * 4 + kc
                            MM(p1.t[:, :], wbr.t[:, kk * 512 + j * 128:kk * 512 + (j + 1) * 128], Ys[b][kc].t[:, :],
                               kc == 0, kc == 3, [wbr.B, Ys[b][kc].B], p1.B)
                        p2 = PSP.alloc()
                        dense(p2, wg, 8, 512, j, hb)
                        G = FP.alloc()
                        ACT(G.t[:, :], p2.t[:, :], AF.Sigmoid, [p2.B, BVEC], [G.B], bias=vcol("gate_b", b * 8 + oc))
                        PSP.release(p2)
                        if b == 0:
                            TTo("dve", M[j].t[:, :], p1.t[:, :], G.t[:, :], ALU.mult, [p1.B, G.B], [M[j].B])
                        else:
                            TTo("dve", G.t[:, :], p1.t[:, :], G.t[:, :], ALU.mult, [p1.B, G.B], [G.B])
                            if b < 3:
                                TTo("pool", M[j].t[:, :], M[j].t[:, :], G.t[:, :], ALU.add, [M[j].B, G.B], [M[j].B])
                            else:
                                mb = BP.alloc()
                                TTo("pool", mb.t[:, :], M[j].t[:, :], G.t[:, :], ALU.add, [M[j].B, G.B], [mb.B])
                                MB[oc] = mb
                        PSP.release(p1)
                        FP.release(G)
                    WDONE(wg)
                    if b % 2 == 1:
                        WDONE(wbr)
                for u in M:
                    FP.release(u)
            for Y in Ys:
                for u in Y:
                    BP.release(u)
            for u in hb:
                BP.release(u)

            if t >= 1 and STOP <= 5:
                return
            MARKS.append((l, t, "wo", P.n_mm))
            MO = []
            for og in range(2):
                wo = WNEXT("wo")
                for j in range(4):
                    psu = PSP.alloc()
                    dense(psu, wo, 8, 512, j, MB)
                    mo = FP.alloc()
                    CP("act", mo.t[:, :], psu.t[:, :], [psu.B], [mo.B])
                    PSP.release(psu)
                    MO.append(mo)
                WDONE(wo)
            for u in MB:
                BP.release(u)
            residual_add(l, MO, "g_post_mix")

            if t >= 1 and STOP <= 6:
                return
            MARKS.append((l, t, "ffn", P.n_mm))
            hb = norm_to_bf16(l, "g_pre_ffn")
            ACTS = []
            for jg in range(6):
                ncols = 512 if jg < 5 else 256
                wgt = WNEXT("fg")
                wup = WNEXT("fu")
                for j in range(ncols // 128):
                    pg = PSP.alloc()
                    dense(pg, wgt, 8, ncols, j, hb)
                    pu = PSP.alloc()
                    dense(pu, wup, 8, ncols, j, hb)
                    sg = FP.alloc()
                    ACT(sg.t[:, :], pg.t[:, :], AF.Silu, [pg.B], [sg.B])
                    PSP.release(pg)
                    a = BP.alloc()
                    TTo("dve", a.t[:, :], pu.t[:, :], sg.t[:, :], ALU.mult, [pu.B, sg.B], [a.B])
                    PSP.release(pu)
                    FP.release(sg)
                    ACTS.append(a)
                WDONE(wgt)
                WDONE(wup)
            for u in hb:
                BP.release(u)
            FO = []
            for og in range(2):
                ps4 = [PSP.alloc() for _ in range(4)]
                for kg in range(3):
                    nk = 8 if kg < 2 else 6
                    wfo = WNEXT("fo")
                    for j in range(4):
                        dense(ps4[j], wfo, nk, 512, j, ACTS, k0=kg * 8, start=(kg == 0), stop=(kg == 2))
                    WDONE(wfo)
                for j in range(4):
                    fo = FP.alloc()
                    CP("act", fo.t[:, :], ps4[j].t[:, :], [ps4[j].B], [fo.B])
                    PSP.release(ps4[j])
                    FO.append(fo)
            for u in ACTS:
                BP.release(u)
            residual_add(l, FO, "g_post_ffn")

            if t >= 1 and STOP <= 7:
                return
            MARKS.append((l, t, "ple", P.n_mm))
            pf = [FP.alloc(), FP.alloc()]
            P.dma("sync", [(pf[k].t[:, :], pT_h.ap()[l, k * 128:(k + 1) * 128, tsl]) for k in range(2)],
                  writes=[pf[0].B, pf[1].B])
            pb = []
            for k in range(2):
                u = BP.alloc()
                CP("pool", u.t[:, :], pf[k].t[:, :], [pf[k].B], [u.B])
                FP.release(pf[k])
                pb.append(u)
            hb = norm_to_bf16(l, "g_ple")
            wpi = WNEXT("pi")
            for og in range(2):
                wpg = WNEXT("pg")
                for j in range(4):
                    oc = og * 4 + j
                    pe_ = PSP.alloc()
                    for kc in range(2):
                        MM(pe_.t[:, :], wpi.t[:, kc * 1024 + oc * 128:kc * 1024 + (oc + 1) * 128], pb[kc].t[:, :],
                           kc == 0, kc == 1, [wpi.B, pb[kc].B], pe_.B)
                    pg = PSP.alloc()
                    dense(pg, wpg, 8, 512, j, hb)
                    ge = FP.alloc()
                    ACT(ge.t[:, :], pg.t[:, :], AF.Sigmoid, [pg.B], [ge.B])
                    PSP.release(pg)
                    TTo("dve", ge.t[:, :], pe_.t[:, :], ge.t[:, :], ALU.mult, [pe_.B, ge.B], [ge.B])
                    PSP.release(pe_)
                    TTo("pool", xs[oc].t[:, :], xs[oc].t[:, :], ge.t[:, :], ALU.add, [xs[oc].B, ge.B], [xs[oc].B])
                    FP.release(ge)
                WDONE(wpg)
            WDONE(wpi)
            for u in hb:
                BP.release(u)
            for u in pb:
                BP.release(u)
            if t >= 1 and STOP <= 8:
                return
            MARKS.append((l, t, "store", P.n_mm))
            P.dma("sync", [(xout_ap[c * 128:(c + 1) * 128, tsl], xs[c].t[:, :]) for c in range(NCH)],
                  reads=[u.B for u in xs], writes=[Bxout])

        Bxin0 = Buf("xin")
        BDBG = Buf("dbg")
        for l in range(NL):
            for t in range(NTE):
                if l == 0:
                    xin_ap, Bxin = xT_h.ap(), Bxin0
                else:
                    xin_ap, Bxin = xmid_h.ap(), BXMID[t]
                if l == NL - 1:
                    xout_ap, Bxout = out_h.ap(), BOUT[t]
                else:
                    xout_ap, Bxout = xmid_h.ap(), BXMID[t]
                emit_tile(l, t, xin_ap, Bxin, xout_ap, Bxout)
        P.wait_all("sync", BOUT[:NTE] + ([BDBG] if debug else []))
        info = dict(marks=MARKS, n_op=P.n_op, n_wait=P.n_wait, nsem=P.nsem, fmin=FP.minfree, bmin=BP.minfree, pmin=PSP.minfree)
        P.run()
    return nc, info


def _lam_init(l):
    return 0.8 - 0.6 * math.exp(-0.3 * l)


_PROG_CACHE = {}


def _get_prog(NL, lam_inits, debug=False):
    key = (NL, tuple(lam_inits), debug)
    if key not in _PROG_CACHE:
        _PROG_CACHE[key] = build_program(NL, lam_inits, debug)
    return _PROG_CACHE[key]


def _layer_inputs(inp, layers):
    ls = list(layers)
    f = lambda a: np.ascontiguousarray(np.asarray(a, np.float32))
    onehot, mask, J = _consts()
    d = dict(
        rel_bias=f(inp["rel_bias"]),
        c_onehot=onehot, c_mask=mask, c_J=J,
        vecs=np.stack([_pack_vecs(inp, l) for l in ls]),
        lru_bd=np.stack([_pack_lru_bd(inp, l) for l in ls]),
        w_in=f(np.asarray(inp["w_in"])[ls]),
        w_branch=f(np.asarray(inp["w_branch"])[ls]).reshape(len(ls), 2048, D),
        w_o=f(np.asarray(inp["w_o"])[ls]),
        w_ffn_in=f(np.asarray(inp["w_ffn_in"])[ls]),
        w_ffn_out=f(np.asarray(inp["w_ffn_out"])[ls]),
        w_ple_in=f(np.asarray(inp["w_ple_in"])[ls]),
        w_ple_gate=f(np.asarray(inp["w_ple_gate"])[ls]),
    )
    return d


def kernel(**inputs):
    x = np.asarray(inputs["x"], np.float32)
    p = np.asarray(inputs["p"], np.float32)
    B = x.shape[0]
    assert B == NCORES
    if FUSED:
        groups = [[0, 1]]
    else:
        groups = [[0], [1]]
    xT = [np.ascontiguousarray(x[b].T) for b in range(B)]
    for ls in groups:
        shared = _layer_inputs(inputs, ls)
        nc, _ = _get_prog(len(ls), [_lam_init(l) for l in ls])
        in_maps = []
        for b in range(B):
            m = dict(shared)
            m["xT"] = xT[b]
            m["pT"] = np.ascontiguousarray(np.stack([p[l, b].T for l in ls]))
            in_maps.append(m)
        res = run_bass_kernel_spmd(nc, in_maps, core_ids=list(range(NCORES)))
        xT = [np.asarray(res.results[b]["outT"], np.float32) for b in range(B)]
    out = np.stack([xT[b].T for b in range(B)]).astype(np.float32)
    return np.ascontiguousarray(out)
```
